# Optimizing a Trainium2 kernel written in Bass

```python
import jax, jax.numpy as jnp
from jax import lax
import numpy as np

D_MODEL = 2048
BATCH = 4
SEQ = 4096
DEPTH = 1

N_HEADS_MLA = 8
Q_LORA_RANK = 512
KV_LORA_RANK = 512
QK_NOPE_DIM = 128
QK_ROPE_DIM = 64
QK_HEAD_DIM = QK_NOPE_DIM + QK_ROPE_DIM
V_HEAD_DIM = 128
MLA_WIDTH = N_HEADS_MLA * V_HEAD_DIM
N_HEADS_SB = 8
SB_HEAD_DIM = 128
SB_WIDTH = N_HEADS_SB * SB_HEAD_DIM
D_FF = -(-8 * D_MODEL // (3 * 256)) * 256
D_IN = Q_LORA_RANK + KV_LORA_RANK + QK_ROPE_DIM + 3 * SB_WIDTH + 2 * D_MODEL
Q_BLOCK = 128
ROPE_THETA = 10000.0
EPS = 1e-6

kernel_name = "hybrid_mla_stickbreaking_gated_block"


def _rms(x, g):
    xf = x.astype(jnp.float32)
    y = xf * lax.rsqrt(jnp.mean(xf * xf, axis=-1, keepdims=True) + EPS)
    return (y * g.astype(jnp.float32)).astype(x.dtype)


def _rope(x, pos):
    half = x.shape[-1] // 2
    freqs = ROPE_THETA ** (-jnp.arange(half, dtype=jnp.float32) / half)
    ang = pos.astype(jnp.float32)[..., None] * freqs
    cos = jnp.cos(ang)[:, :, None, :]
    sin = jnp.sin(ang)[:, :, None, :]
    xf = x.astype(jnp.float32)
    x1, x2 = xf[..., :half], xf[..., half:]
    return jnp.concatenate([x1 * cos - x2 * sin, x1 * sin + x2 * cos], axis=-1).astype(x.dtype)


def _mla_attention(q, k, v):
    S = q.shape[2]
    scale = QK_HEAD_DIM ** -0.5
    outs = []
    for i in range(S // Q_BLOCK):
        end = (i + 1) * Q_BLOCK
        qb = q[:, :, i * Q_BLOCK:end]
        s = jnp.einsum('bhqd,bhkd->bhqk', qb, k[:, :, :end]).astype(jnp.float32) * scale
        qi = i * Q_BLOCK + jnp.arange(Q_BLOCK)
        ki = jnp.arange(end)
        s = jnp.where(ki[None, :] <= qi[:, None], s, -jnp.inf)
        p = jax.nn.softmax(s, axis=-1).astype(v.dtype)
        outs.append(jnp.einsum('bhqk,bhkd->bhqd', p, v[:, :, :end]))
    return jnp.concatenate(outs, axis=2)


def _stick_breaking(q, k, v):
    S = q.shape[2]
    scale = SB_HEAD_DIM ** -0.5
    outs = []
    for i in range(S // Q_BLOCK):
        end = (i + 1) * Q_BLOCK
        qb = q[:, :, i * Q_BLOCK:end]
        z = jnp.einsum('bhqd,bhkd->bhqk', qb, k[:, :, :end]).astype(jnp.float32) * scale
        qi = i * Q_BLOCK + jnp.arange(Q_BLOCK)
        ki = jnp.arange(end)
        mask = ki[None, :] < qi[:, None]
        log_beta = jax.nn.log_sigmoid(z)
        log_one_minus = jnp.where(mask, jax.nn.log_sigmoid(-z), 0.0)
        tail = lax.cumsum(log_one_minus, axis=3, reverse=True) - log_one_minus
        a = jnp.where(mask, jnp.exp(log_beta + tail), 0.0).astype(v.dtype)
        outs.append(jnp.einsum('bhqk,bhkd->bhqd', a, v[:, :, :end]))
    return jnp.concatenate(outs, axis=2)


def _layer(x, c_act, pos, w_ada, b_ada, g_norm1, g_norm2, w_in, g_q_latent, g_kv_latent,
           w_uq, w_ukv, g_q_head, g_k_head, w_proj_mla, w_proj_sb, w_out, w_ffn_in, w_ffn_out):
    B, S, _ = x.shape
    ada = (c_act @ w_ada + b_ada)[:, None, :]
    sh1, sc1, gt1, sh2, sc2, gt2 = jnp.split(ada, 6, axis=-1)

    h = _rms(x, g_norm1) * (1 + sc1) + sh1
    proj = h @ w_in
    offs = np.cumsum([Q_LORA_RANK, KV_LORA_RANK, QK_ROPE_DIM, SB_WIDTH, SB_WIDTH, SB_WIDTH, D_MODEL])
    c_q, c_kv, k_pe, q_sb, k_sb, v_sb, gl_a, gl_b = jnp.split(proj, [int(o) for o in offs], axis=-1)

    q = (_rms(c_q, g_q_latent) @ w_uq).reshape(B, S, N_HEADS_MLA, QK_HEAD_DIM)
    kv = (_rms(c_kv, g_kv_latent) @ w_ukv).reshape(B, S, N_HEADS_MLA, QK_NOPE_DIM + V_HEAD_DIM)
    k_nope, v = kv[..., :QK_NOPE_DIM], kv[..., QK_NOPE_DIM:]
    k_pe_h = jnp.broadcast_to(k_pe[:, :, None, :], (B, S, N_HEADS_MLA, QK_ROPE_DIM))
    k = jnp.concatenate([k_nope, k_pe_h], axis=-1)
    q = _rms(q, g_q_head)
    k = _rms(k, g_k_head)
    q = jnp.concatenate([q[..., :QK_NOPE_DIM], _rope(q[..., QK_NOPE_DIM:], pos)], axis=-1)
    k = jnp.concatenate([k[..., :QK_NOPE_DIM], _rope(k[..., QK_NOPE_DIM:], pos)], axis=-1)
    y_a = _mla_attention(q.transpose(0, 2, 1, 3), k.transpose(0, 2, 1, 3), v.transpose(0, 2, 1, 3))
    y_a = y_a.transpose(0, 2, 1, 3).reshape(B, S, MLA_WIDTH)

    to_heads = lambda t: t.reshape(B, S, N_HEADS_SB, SB_HEAD_DIM).transpose(0, 2, 1, 3)
    y_b = _stick_breaking(to_heads(q_sb), to_heads(k_sb), to_heads(v_sb))
    y_b = y_b.transpose(0, 2, 1, 3).reshape(B, S, SB_WIDTH)

    merged = jax.nn.sigmoid(gl_a) * (y_a @ w_proj_mla) + jax.nn.sigmoid(gl_b) * (y_b @ w_proj_sb)
    x = x + gt1 * (merged @ w_out)

    h2 = _rms(x, g_norm2) * (1 + sc2) + sh2
    gate, up = jnp.split(h2 @ w_ffn_in, 2, axis=-1)
    x = x + gt2 * ((jax.nn.silu(gate) * up) @ w_ffn_out)
    return x


def setup_inputs(seed: int = 0) -> dict:
    key = jax.random.key(seed)
    ks = jax.random.split(key, 24)
    f32 = jnp.float32

    def nrm(k, shape, fan_in):
        return jax.random.normal(k, shape, f32) * (fan_in ** -0.5)

    def gain(k, n):
        return 1.0 + 0.02 * jax.random.normal(k, (DEPTH, n), f32)

    x = jax.random.normal(ks[0], (BATCH, SEQ, D_MODEL), f32)
    c = jax.random.normal(ks[1], (BATCH, D_MODEL), f32)
    offset = jax.random.randint(ks[2], (BATCH, 1), 0, 1024, dtype=jnp.int32)
    positions = offset + jnp.arange(SEQ, dtype=jnp.int32)[None, :]
    return {
        "x": x,
        "c": c,
        "positions": positions,
        "w_ada": nrm(ks[3], (DEPTH, D_MODEL, 6 * D_MODEL), D_MODEL),
        "b_ada": 0.02 * jax.random.normal(ks[4], (DEPTH, 6 * D_MODEL), f32),
        "g_norm1": gain(ks[5], D_MODEL),
        "g_norm2": gain(ks[6], D_MODEL),
        "w_in": nrm(ks[7], (DEPTH, D_MODEL, D_IN), D_MODEL),
        "g_q_latent": gain(ks[8], Q_LORA_RANK),
        "g_kv_latent": gain(ks[9], KV_LORA_RANK),
        "w_uq": nrm(ks[10], (DEPTH, Q_LORA_RANK, N_HEADS_MLA * QK_HEAD_DIM), Q_LORA_RANK),
        "w_ukv": nrm(ks[11], (DEPTH, KV_LORA_RANK, N_HEADS_MLA * (QK_NOPE_DIM + V_HEAD_DIM)), KV_LORA_RANK),
        "g_q_head": gain(ks[12], QK_HEAD_DIM),
        "g_k_head": gain(ks[13], QK_HEAD_DIM),
        "w_proj_mla": nrm(ks[14], (DEPTH, MLA_WIDTH, D_MODEL), MLA_WIDTH),
        "w_proj_sb": nrm(ks[15], (DEPTH, SB_WIDTH, D_MODEL), SB_WIDTH),
        "w_out": nrm(ks[16], (DEPTH, D_MODEL, D_MODEL), D_MODEL),
        "w_ffn_in": nrm(ks[17], (DEPTH, D_MODEL, 2 * D_FF), D_MODEL),
        "w_ffn_out": nrm(ks[18], (DEPTH, D_FF, D_MODEL), D_FF),
    }


def reference(x, c, positions, w_ada, b_ada, g_norm1, g_norm2, w_in, g_q_latent, g_kv_latent,
              w_uq, w_ukv, g_q_head, g_k_head, w_proj_mla, w_proj_sb, w_out, w_ffn_in, w_ffn_out):
    c_act = jax.nn.silu(c)
    for l in range(DEPTH):
        x = _layer(x, c_act, positions, w_ada[l], b_ada[l], g_norm1[l], g_norm2[l], w_in[l],
                   g_q_latent[l], g_kv_latent[l], w_uq[l], w_ukv[l], g_q_head[l], g_k_head[l],
                   w_proj_mla[l], w_proj_sb[l], w_out[l], w_ffn_in[l], w_ffn_out[l])
    return x
```

```python
from contextlib import ExitStack
import math
import numpy as np
import ml_dtypes
import concourse.bass as bass
import concourse.mybir as mybir
from concourse.bass_utils import run_bass_kernel_spmd

F32 = mybir.dt.float32
BF16 = mybir.dt.bfloat16
I32 = mybir.dt.int32
ALU = mybir.AluOpType
AF = mybir.ActivationFunctionType

D = 2048
H = 8
DFF = 5632
NC16 = 16
NFF = 44
EPS = 1e-6
SC_SB = 128 ** -0.5
SC_MLA = 192 ** -0.5
TWO_PI = 2.0 * math.pi
CW1 = 6.28125
CW2 = TWO_PI - CW1


class Buf:
    def __init__(self, name, t, space):
        self.name, self.t, self.space = name, t, space
        self.last_w = None
        self.readers = []
        self.aliases = []

    def __getitem__(self, idx):
        return self.t[idx]


class Op:
    __slots__ = ("eng", "fn", "waits", "idx", "inc", "dkey")

    def __init__(self, eng, fn, waits, idx, dkey=None):
        self.eng, self.fn, self.waits, self.idx, self.dkey = eng, fn, waits, idx, dkey
        self.inc = False


class SemPool:
    ENGS = ("pe", "act", "dve", "pool", "sp")

    def __init__(self, nc, stack):
        self.nc, self.stack = nc, stack
        self.esem = {e: stack.enter_context(nc.semaphore("tl_" + e)) for e in self.ENGS}
        self.ebase = {e: 0 for e in self.ENGS}
        self.dsem = []
        self.dbase = []
        self.free = {"hw": [], "sw": []}

    def dslot(self, kind, i):
        lst = self.free[kind]
        while len(lst) <= i:
            lst.append(len(self.dsem))
            self.dsem.append(self.stack.enter_context(self.nc.semaphore("dq%s%d" % (kind, len(self.dsem)))))
            self.dbase.append(0)
        return lst[i]


class K:
    ENGS = SemPool.ENGS

    _n = [0]

    def __init__(self, nc, stack, pool):
        self.nc, self.stack, self.pool = nc, stack, pool
        K._n[0] += 1
        self.pfx = "k%d_" % K._n[0]
        self.ops = {e: [] for e in self.ENGS}
        self.dcount = {}
        self.dkeys = []
        self.dkind = {}

    def sb(self, name, shape, dtype):
        return Buf(name, self.stack.enter_context(self.nc.sbuf_tensor(self.pfx + name, list(shape), dtype)), "sb")

    def wrap(self, name, t, space):
        return Buf(name, t, space)

    def alias(self, *bufs):
        for b in bufs:
            b.aliases = [o for o in bufs if o is not b]

    def _collect(self, eng, reads, writes):
        ev = []
        for b in reads:
            if b.last_w is not None:
                ev.append(b.last_w)
        for b in writes:
            for bb in [b] + b.aliases:
                if bb.last_w is not None:
                    ev.append(bb.last_w)
                ev.extend(bb.readers)
        out, seen = [], set()
        for e in ev:
            if e[0] == "d":
                e = ("d", e[1], self.dcount[e[1]])
            if e[0] == "e" and e[1] == eng and eng in ("pe", "sp"):
                continue
            if e in seen:
                continue
            seen.add(e)
            out.append(e)
        return out

    def _commit(self, ev, reads, writes):
        for b in reads:
            b.readers.append(ev)
        for b in writes:
            b.last_w = ev
            b.readers = []

    def op(self, eng, fn, reads=(), writes=()):
        waits = self._collect(eng, reads, writes)
        idx = len(self.ops[eng])
        self.ops[eng].append(Op(eng, fn, waits, idx))
        self._commit(("e", eng, idx), reads, writes)

    def dma(self, q, fn, reads=(), writes=(), key=None):
        if key is None:
            cands = [b for b in list(writes) + list(reads) if b.space != "dram"]
            key = (cands[0] if cands else (list(writes) + list(reads))[0]).name
        waits = self._collect(q, reads, writes)
        idx = len(self.ops[q])
        kind = "sw" if q == "pool" else "hw"
        if kind == "sw":
            key = "swA" if key in ("sc_k", "sc_q", "sc_y", "out") else "swB"
        if key not in self.dcount:
            self.dcount[key] = 0
            self.dkeys.append(key)
            self.dkind[key] = kind
        assert self.dkind[key] == kind, key
        self.dcount[key] += 1
        self.ops[q].append(Op(q, fn, waits, idx, dkey=key))
        self._commit(("d", key, self.dcount[key]), reads, writes)

    def emit(self):
        nc, pool = self.nc, self.pool
        for e in self.ENGS:
            for o in self.ops[e]:
                for w in o.waits:
                    if w[0] == "e":
                        self.ops[w[1]][w[2]].inc = True
        semval = {}
        newbase = {}
        for e in self.ENGS:
            c = pool.ebase[e]
            for o in self.ops[e]:
                if o.inc:
                    c += 1
                    semval[(e, o.idx)] = c
            newbase[e] = c
        slot = {}
        nk = {"hw": 0, "sw": 0}
        for k in self.dkeys:
            slot[k] = pool.dslot(self.dkind[k], nk[self.dkind[k]])
            nk[self.dkind[k]] += 1
        dbase = {k: pool.dbase[slot[k]] for k in self.dkeys}
        esem, dsem = pool.esem, pool.dsem
        block = self.stack.enter_context(nc.Block())

        def make(ename):
            ops = self.ops[ename]

            def body(eng):
                waited = {}
                for o in ops:
                    for w in o.waits:
                        if w[0] == "e":
                            sem, val, kk = esem[w[1]], semval[(w[1], w[2])], ("e", w[1])
                        else:
                            sem, val, kk = dsem[slot[w[1]]], dbase[w[1]] + 16 * w[2], ("d", w[1])
                        if waited.get(kk, 0) >= val:
                            continue
                        waited[kk] = val
                        eng.wait_ge(sem, val)
                    ins = o.fn(eng)
                    if o.dkey is not None:
                        ins.then_inc(dsem[slot[o.dkey]], 16)
                    elif o.inc:
                        ins.then_inc(esem[ename], 1)
                if ename == "sp":
                    for k in self.dkeys:
                        eng.wait_ge(dsem[slot[k]], dbase[k] + 16 * self.dcount[k])
            return body

        block.sync(make("sp"))
        block.tensor(make("pe"))
        block.scalar(make("act"))
        block.vector(make("dve"))
        block.gpsimd(make("pool"))
        for e in self.ENGS:
            pool.ebase[e] = newbase[e]
        for k in self.dkeys:
            pool.dbase[slot[k]] += 16 * self.dcount[k]
        self.n_ops = {e: len(self.ops[e]) for e in self.ENGS}


class Cfg:
    def __init__(self, S=4096, TT=512, debug=False):
        self.S, self.TT, self.debug = S, TT, debug
        self.NB = TT // 128
        self.NT = S // TT
        assert self.NT == 8
        self.NO = 4
        self.SO = S // 2
        self.NBLK = S // 128
        self.own = {0: [0, 3, 4, 7], 1: [1, 2, 5, 6]}
        self.nkt = [2, 4, 6, 8]


CV = {}
_o = 0
for _n, _w in [("c", 16), ("bada", 96), ("g1", 16), ("g2", 16), ("gq", 4), ("gkv", 4),
               ("gqh", 1), ("gkh", 1), ("gqpe", 1), ("gqpes", 1), ("gkpe", 1), ("gkpes", 1), ("freq", 1)]:
    CV[_n] = (_o, _o + _w)
    _o += _w
NCV = _o


def cvs(cv, name, i=None):
    a, b = CV[name]
    if i is None:
        return cv[:, a:b]
    return cv[:, a + i:a + i + 1]


class Prog:
    def __init__(self, cfg):
        self.cfg = cfg
        self.nc = bass.Bass("TRN2", target_bir_lowering=False)
        self.stats = {}

    def din(self, name, shape, dtype):
        return self.nc.dram_tensor(name, list(shape), dtype, kind="ExternalInput").ap()

    def dscr(self, name, shape, dtype):
        kind = "ExternalOutput" if self.cfg.debug else "Internal"
        return self.nc.dram_tensor(name, list(shape), dtype, kind=kind).ap()

    def build(self):
        cfg, nc = self.cfg, self.nc
        S, TT, NB, SO, NBLK = cfg.S, cfg.TT, cfg.NB, cfg.SO, cfg.NBLK
        specs = {"xa": ([S, D], F32), "xo": ([SO, D], F32), "pa": ([64, S], I32), "po": ([64, SO], I32),
                 "cv": ([128, NCV], F32), "idf": ([128, 128], F32), "tri": ([128, 128], BF16),
                 "mks": ([128, 2 * 2 * NB * TT], BF16), "mki": ([128, 2 * 2 * NB * TT], BF16),
                 "w_ada": ([96, 128, 2048], F32)}
        for n, x in [("w_ksb", 16 * 1024), ("w_vsb", 16 * 1024), ("w_qsb", 16 * 1024), ("w_ckv", 16 * 512),
                     ("w_kpe", 16 * 256), ("w_cq", 16 * 512), ("w_ukn", 4 * 1024), ("w_ukv", 4 * 1024),
                     ("w_uq", 4 * 3072)]:
            specs[n] = ([128, x], F32)
        for n, shp in [("w_ga", [16, 128, 2048]), ("w_gb", [16, 128, 2048]), ("w_pa", [16, 128, 1024]),
                       ("w_pb", [16, 128, 1024]), ("w_o", [16, 128, 2048]), ("w_f1", [88, 128, 2048]),
                       ("w_f2", [64, 128, 11 * 128])]:
            specs[n] = (shp, F32)
        prog = self

        class Lazy(dict):
            def __missing__(self, name):
                shp, dt = specs[name]
                self[name] = prog.din(name, shp, dt)
                return self[name]
        I = Lazy()
        self.I = I
        self.out = nc.dram_tensor("out", [SO, D], F32, kind="ExternalOutput").ap()
        Sc = {}
        Sc["ksb"] = self.dscr("s_ksb", [H, 128, S], BF16)
        Sc["vsb"] = self.dscr("s_vsb", [H, 128, NBLK, 128], BF16)
        Sc["qsb"] = self.dscr("s_qsb", [H, 128, SO], BF16)
        Sc["kn"] = self.dscr("s_kn", [H, 128, S], BF16)
        Sc["kp"] = self.dscr("s_kp", [H, 128, S], BF16)
        Sc["vm"] = self.dscr("s_vm", [H, 128, NBLK, 128], BF16)
        Sc["qn"] = self.dscr("s_qn", [H, 128, SO], BF16)
        Sc["qp"] = self.dscr("s_qp", [H, 128, SO], BF16)
        Sc["ya"] = self.dscr("s_ya", [H, 128, SO], BF16)
        Sc["yb"] = self.dscr("s_yb", [H, 128, SO], BF16)
        if cfg.debug:
            Sc["gc"] = self.dscr("s_gc", [128, 96], F32)
            Sc["dbg"] = self.dscr("s_dbg", [128, 3, TT], F32)
            Sc["dx1"] = self.dscr("s_dx1", [4, 128, 16 * TT], F32)
            Sc["dG"] = self.dscr("s_dG", [4, 128, NFF * TT], BF16)
            Sc["dh2"] = self.dscr("s_dh2", [4, 128, 16 * TT], BF16)
            Sc["dmg"] = self.dscr("s_dmg", [4, 128, 16 * TT], BF16)
        self.Sc = Sc

        with ExitStack() as gst:
            self.gst = gst
            self.pool = SemPool(nc, gst)
            self.psum = [gst.enter_context(nc.psum_tensor("ps%d" % i, [128, 512], F32)) for i in range(8)]
            g = {}
            g["cv"] = gst.enter_context(nc.sbuf_tensor("g_cv", [128, NCV], F32))
            g["gc"] = gst.enter_context(nc.sbuf_tensor("g_gc", [128, 96], F32))
            g["idf"] = gst.enter_context(nc.sbuf_tensor("g_idf", [128, 128], F32))
            g["idb"] = gst.enter_context(nc.sbuf_tensor("g_idb", [128, 128], BF16))
            g["ones"] = gst.enter_context(nc.sbuf_tensor("g_ones", [128, 128], BF16))
            g["tri"] = gst.enter_context(nc.sbuf_tensor("g_tri", [128, 128], BF16))
            self.g = g
            stages = getattr(cfg, "stages", "012345")
            self.stage0(ada=("0" in stages))
            if "1" in stages:
                self.stage1("sb")
            if "2" in stages:
                self.stage2_sb()
            if "3" in stages:
                self.stage1("mla")
            if "4" in stages:
                self.stage2_mla()
            if "5" in stages:
                self.stage3()
        return nc

    def newk(self, st):
        k = K(self.nc, st, self.pool)
        G = {n: k.wrap(n, t, "sb") for n, t in self.g.items()}
        P = [k.wrap("ps%d" % i, t, "ps") for i, t in enumerate(self.psum)]
        return k, G, P

    def dr(self, k, name, ap):
        return k.wrap(name, ap, "dram")

    def stage0(self, ada=True):
        nc, I = self.nc, self.I
        with ExitStack() as st:
            k, G, P = self.newk(st)
            cvd, idfd, trid = (self.dr(k, n, I[n]) for n in ("cv", "idf", "tri"))
            wad = self.dr(k, "w_ada", I["w_ada"]) if ada else None
            k.dma("sp", lambda e: e.dma_start(out=G["cv"][:], in_=cvd[:]), reads=[cvd], writes=[G["cv"]])
            k.dma("sp", lambda e: e.dma_start(out=G["idf"][:], in_=idfd[:]), reads=[idfd], writes=[G["idf"]])
            k.dma("sp", lambda e: e.dma_start(out=G["tri"][:], in_=trid[:]), reads=[trid], writes=[G["tri"]])
            k.op("dve", lambda e: e.tensor_copy(out=G["idb"][:], in_=G["idf"][:]), reads=[G["idf"]], writes=[G["idb"]])
            k.op("dve", lambda e: e.memset(G["ones"][:], 1.0), writes=[G["ones"]])
            if not ada:
                k.op("dve", lambda e: e.memset(G["gc"][:], 1.0), writes=[G["gc"]])
                k.emit()
                return
            cact = k.sb("cact", [128, 16], F32)
            k.op("act", lambda e: e.activation(out=cact[:], in_=cvs(G["cv"], "c"), func=AF.Silu),
                 reads=[G["cv"]], writes=[cact])
            wst = [k.sb("wst%d" % i, [128, 2048], F32) for i in range(3)]
            ps = P[0]
            for fb in range(96):
                w = wst[fb % 3]
                k.dma("sp", lambda e, w=w, fb=fb: e.dma_start(out=w[:], in_=wad[fb]), reads=[wad], writes=[w])
                for c in range(16):
                    k.op("pe", lambda e, w=w, fb=fb, c=c: e.matmul(
                        out=ps[:, fb:fb + 1], lhsT=w[:, c * 128:(c + 1) * 128], rhs=cact[:, c:c + 1],
                        start=(c == 0), stop=(c == 15)), reads=[w, cact], writes=[ps])
            ada = k.sb("ada", [128, 96], F32)
            k.op("dve", lambda e: e.tensor_tensor(out=ada[:], in0=ps[:, 0:96], in1=cvs(G["cv"], "bada"), op=ALU.add),
                 reads=[ps, G["cv"]], writes=[ada])
            gc = G["gc"]
            k.op("dve", lambda e: e.scalar_tensor_tensor(out=gc[:, 0:16], in0=ada[:, 16:32], scalar=1.0,
                                                         in1=cvs(G["cv"], "g1"), op0=ALU.add, op1=ALU.mult),
                 reads=[ada, G["cv"]], writes=[gc])
            k.op("dve", lambda e: e.scalar_tensor_tensor(out=gc[:, 48:64], in0=ada[:, 64:80], scalar=1.0,
                                                         in1=cvs(G["cv"], "g2"), op0=ALU.add, op1=ALU.mult),
                 reads=[ada, G["cv"]], writes=[gc])
            for (a, b, c) in [(16, 0, 16), (32, 32, 16), (64, 48, 16), (80, 80, 16)]:
                k.op("dve", lambda e, a=a, b=b, c=c: e.tensor_copy(out=gc[:, a:a + c], in_=ada[:, b:b + c]),
                     reads=[ada], writes=[gc])
            if self.cfg.debug:
                gcd = self.dr(k, "s_gc", self.Sc["gc"])
                k.dma("pool", lambda e: e.dma_start(out=gcd[:], in_=gc[:]), reads=[gc], writes=[gcd])
            k.emit()
            self.stats["s0"] = k.n_ops

    def load_resident(self, k, wd, dst, ncols, stg, cnt):
        off = 0
        while off < ncols:
            n = min(2048, ncols - off)
            s = stg[cnt[0] % len(stg)]
            eng = "pool" if cnt[0] % 2 == 0 else "dve"
            cnt[0] += 1
            k.dma("sp", lambda e, s=s, off=off, n=n: e.dma_start(out=s[:, 0:n], in_=wd[:, off:off + n]),
                  reads=[wd], writes=[s])
            k.op(eng, lambda e, s=s, off=off, n=n: e.tensor_copy(out=dst[:, off:off + n], in_=s[:, 0:n]),
                 reads=[s], writes=[dst])
            off += n

    def prologue(self, k, G, xd, row0, B, tpb, f32T=None, tpf=None, gcoff=0):
        TT, NB = self.cfg.TT, self.cfg.NB
        xblk, xn, junk, ss, rs, hT = B["xblk"], B["xn"], B["junk"], B["ss"], B["rs"], B["hT"]
        gc = G["gc"]
        for blk in range(NB):
            xb = xblk[blk % 2]
            r0 = row0 + blk * 128
            k.dma("sp", lambda e, xb=xb, r0=r0: e.dma_start(out=xb[:], in_=xd[r0:r0 + 128, :]), reads=[xd], writes=[xb])
            k.op("act", lambda e, xb=xb, blk=blk: e.activation(out=junk[:], in_=xb[:], func=AF.Square,
                                                               accum_out=ss[:, blk:blk + 1]),
                 reads=[xb], writes=[junk, ss])
            k.op("act", lambda e, blk=blk: e.activation(out=rs[:, blk:blk + 1], in_=ss[:, blk:blk + 1], func=AF.Ln,
                                                        scale=1.0 / D, bias=EPS), reads=[ss], writes=[rs])
            k.op("act", lambda e, blk=blk: e.activation(out=rs[:, blk:blk + 1], in_=rs[:, blk:blk + 1], func=AF.Exp,
                                                        scale=-0.5), reads=[rs], writes=[rs])
            k.op("act", lambda e, xb=xb, blk=blk: e.activation(out=xn[:, blk, :], in_=xb[:], func=AF.Copy,
                                                               scale=rs[:, blk:blk + 1]),
                 reads=[xb, rs], writes=[xn])
            if f32T is not None:
                for cg in range(4):
                    p = tpf[cg % 2]
                    for j in range(4):
                        c = cg * 4 + j
                        k.op("pe", lambda e, p=p, j=j, c=c, xb=xb: e.transpose(
                            out=p[:, j * 128:(j + 1) * 128], in_=xb[:, c * 128:(c + 1) * 128], identity=G["idf"][:]),
                            reads=[xb, G["idf"]], writes=[p])
                    k.op("dve", lambda e, p=p, cg=cg, blk=blk: e.tensor_copy(
                        out=f32T[:, cg * 4:(cg + 1) * 4, blk * 128:(blk + 1) * 128],
                        in_=p[:, 0:512].rearrange("p (j t) -> p j t", j=4)), reads=[p], writes=[f32T])
        for c in range(16):
            p = tpb[c % 2]
            pv = p[:, 0:TT // 2].bitcast(BF16)
            for blk in range(NB):
                k.op("pe", lambda e, pv=pv, c=c, blk=blk: e.transpose(
                    out=pv[:, blk * 128:(blk + 1) * 128], in_=xn[:, blk, c * 128:(c + 1) * 128], identity=G["idb"][:]),
                    reads=[xn, G["idb"]], writes=[p])
            k.op("dve", lambda e, pv=pv, c=c: e.tensor_scalar(
                out=hT[:, c, :], in0=pv[:, 0:TT], scalar1=gc[:, gcoff + c:gcoff + c + 1],
                scalar2=gc[:, gcoff + 16 + c:gcoff + 17 + c], op0=ALU.mult, op1=ALU.add),
                reads=[p, gc], writes=[hT])

    def prologue_bufs(self, k):
        TT, NB = self.cfg.TT, self.cfg.NB
        B = {}
        B["xblk"] = [k.sb("xblk%d" % i, [128, D], F32) for i in range(2)]
        B["xn"] = k.sb("xn", [128, NB, D], BF16)
        B["junk"] = k.sb("junk", [128, D], BF16)
        B["ss"] = k.sb("ss", [128, NB], F32)
        B["rs"] = k.sb("rs", [128, NB], F32)
        B["hT"] = k.sb("hT", [128, 16, TT], BF16)
        return B

    def rope_tables(self, k, G, posd, col0, T):
        TT = self.cfg.TT
        pi_, pf, ang, kk, r, r2, C2, S2 = (T[n] for n in ("pi", "pf", "ang", "kk", "r", "r2", "C2", "S2"))
        cv = G["cv"]
        k.dma("sp", lambda e: e.dma_start(out=pi_[:], in_=posd[:, col0:col0 + TT]),
              reads=[posd], writes=[pi_])
        k.op("dve", lambda e: e.tensor_copy(out=pf[:], in_=pi_[:]), reads=[pi_], writes=[pf])
        k.op("dve", lambda e: e.tensor_scalar(out=ang[:], in0=pf[:], scalar1=cvs(cv, "freq")[0:64, :], scalar2=None,
                                              op0=ALU.mult), reads=[pf, cv], writes=[ang])

        MAGIC = 12582912.0

        def reduce_(src, dst):
            k.op("dve", lambda e: e.tensor_scalar(out=kk[:], in0=src[:], scalar1=1.0 / TWO_PI, scalar2=MAGIC,
                                                  op0=ALU.mult, op1=ALU.add), reads=[src], writes=[kk])
            k.op("dve", lambda e: e.tensor_scalar(out=kk[:], in0=kk[:], scalar1=-MAGIC, scalar2=None,
                                                  op0=ALU.add), reads=[kk], writes=[kk])
            k.op("dve", lambda e: e.scalar_tensor_tensor(out=r[:], in0=kk[:], scalar=-CW1, in1=src[:],
                                                         op0=ALU.mult, op1=ALU.add), reads=[kk, src], writes=[r])
            k.op("dve", lambda e: e.scalar_tensor_tensor(out=r[:], in0=kk[:], scalar=-CW2, in1=r[:],
                                                         op0=ALU.mult, op1=ALU.add), reads=[kk, r], writes=[r])
            k.op("dve", lambda e: e.tensor_scalar(out=dst[:], in0=r[:], scalar1=math.pi, scalar2=-math.pi,
                                                  op0=ALU.min, op1=ALU.max), reads=[r], writes=[dst])

        reduce_(ang, r2)
        k.op("act", lambda e: e.activation(out=S2[:], in_=r2[:], func=AF.Sin), reads=[r2], writes=[S2])
        k.op("dve", lambda e: e.tensor_scalar(out=S2[0:32, :], in0=S2[0:32, :], scalar1=-1.0, scalar2=None,
                                              op0=ALU.mult), reads=[S2], writes=[S2])
        k.op("dve", lambda e: e.tensor_scalar(out=ang[:], in0=r2[:], scalar1=math.pi / 2, scalar2=None,
                                              op0=ALU.add), reads=[r2], writes=[ang])
        reduce_(ang, r2)
        k.op("act", lambda e: e.activation(out=C2[:], in_=r2[:], func=AF.Sin), reads=[r2], writes=[C2])

    def rope_bufs(self, k):
        TT = self.cfg.TT
        T = {}
        T["pi"] = k.sb("r_pi", [64, TT], I32)
        for n in ("pf", "ang", "kk", "r", "r2", "C2", "S2"):
            T[n] = k.sb("r_" + n, [64, TT], F32)
        return T

    def stage1(self, which):
        cfg, I, Sc = self.cfg, self.I, self.Sc
        TT, NB, S = cfg.TT, cfg.NB, cfg.S
        with ExitStack() as st:
            k, G, P = self.newk(st)
            B = self.prologue_bufs(k)
            hT = B["hT"]
            stg = [k.sb("stg%d" % i, [128, 2048], F32) for i in range(3 if which == "sb" else 2)]
            cnt = [0]
            xa, xo = self.dr(k, "xa", I["xa"]), self.dr(k, "xo", I["xo"])
            tpb = [P[0], P[1]]
            accs = [P[2], P[3], P[4], P[5]]
            ai = [0]

            def nacc():
                ai[0] += 1
                return accs[ai[0] % 4]

            ev = [0]

            def evac_engine():
                ev[0] += 1
                return "act" if ev[0] % 2 == 0 else "dve"

            def fm_group(ps, M, lhs_fn, rhs_fn, nch, extra_reads):
                for c in range(nch):
                    k.op("pe", lambda e, c=c: e.matmul(out=ps[0:M, 0:TT], lhsT=lhs_fn(c), rhs=rhs_fn(c),
                                                       start=(c == 0), stop=(c == nch - 1)),
                         reads=extra_reads, writes=[ps])

            if which == "sb":
                wk = k.sb("wk", [128, 16 * 1024], BF16)
                wv = k.sb("wv", [128, 16 * 1024], BF16)
                self.load_resident(k, self.dr(k, "w_ksb", I["w_ksb"]), wk, 16 * 1024, stg, cnt)
                self.load_resident(k, self.dr(k, "w_vsb", I["w_vsb"]), wv, 16 * 1024, stg, cnt)
                kst = [k.sb("kst%d" % i, [128, TT], BF16) for i in range(3)]
                vt = [k.sb("vt%d" % i, [128, NB, 1024], BF16) for i in range(2)]
                ksd, vsd, qsd = (self.dr(k, n, Sc[n]) for n in ("ksb", "vsb", "qsb"))
                wk3 = wk[:].rearrange("p (c n) -> p c n", c=16)
                wv3 = wv[:].rearrange("p (c n) -> p c n", c=16)
                kc = 0
                hTs = [B["hT"], k.sb("hT_b", [128, 16, TT], BF16)]
                tix = 0
                for kt in range(cfg.NT):
                    hT = hTs[tix % 2]
                    tix += 1
                    B["hT"] = hT
                    self.prologue(k, G, xa, kt * TT, B, tpb)
                    for h in range(H):
                        ps = nacc()
                        fm_group(ps, 128, lambda c, h=h: wk3[:, c, h * 128:(h + 1) * 128], lambda c, hT=hT: hT[:, c, :], 16, [wk, hT])
                        ks = kst[kc % 3]
                        kc += 1
                        eng = evac_engine()
                        if eng == "act":
                            k.op("act", lambda e, ks=ks, ps=ps: e.activation(out=ks[:], in_=ps[:, 0:TT], func=AF.Copy, scale=SC_SB),
                                 reads=[ps], writes=[ks])
                        else:
                            k.op("dve", lambda e, ks=ks, ps=ps: e.tensor_scalar(out=ks[:], in0=ps[:, 0:TT], scalar1=SC_SB, scalar2=None,
                                                                                op0=ALU.mult), reads=[ps], writes=[ks])
                        k.dma("pool", lambda e, ks=ks, h=h, kt=kt: e.dma_start(out=ksd[h, :, kt * TT:(kt + 1) * TT], in_=ks[:]),
                              reads=[ks], writes=[ksd], key="sc_k")
                    v = vt[kt % 2]
                    for blk in range(NB):
                        for nt in range(2):
                            ps = nacc()
                            for c in range(16):
                                k.op("pe", lambda e, c=c, blk=blk, nt=nt, ps=ps, hT=hT: e.matmul(
                                    out=ps[:, 0:512], lhsT=hT[:, c, blk * 128:(blk + 1) * 128],
                                    rhs=wv3[:, c, nt * 512:(nt + 1) * 512], start=(c == 0), stop=(c == 15)),
                                    reads=[hT, wv], writes=[ps])
                            eng = evac_engine()
                            if eng == "act":
                                k.op("act", lambda e, v=v, ps=ps, blk=blk, nt=nt: e.activation(
                                    out=v[:, blk, nt * 512:(nt + 1) * 512], in_=ps[:, 0:512], func=AF.Copy),
                                    reads=[ps], writes=[v])
                            else:
                                k.op("dve", lambda e, v=v, ps=ps, blk=blk, nt=nt: e.tensor_copy(
                                    out=v[:, blk, nt * 512:(nt + 1) * 512], in_=ps[:, 0:512]), reads=[ps], writes=[v])
                    for h in range(H):
                        k.dma("pool", lambda e, v=v, h=h, kt=kt: e.dma_start(
                            out=vsd[h, :, kt * NB:(kt + 1) * NB, :], in_=v[:, :, h * 128:(h + 1) * 128]),
                            reads=[v], writes=[vsd], key="sc_v")
                self.load_resident(k, self.dr(k, "w_qsb", I["w_qsb"]), wk, 16 * 1024, stg, cnt)
                for j in range(cfg.NO):
                    hT = hTs[tix % 2]
                    tix += 1
                    B["hT"] = hT
                    self.prologue(k, G, xo, j * TT, B, tpb)
                    for h in range(H):
                        ps = nacc()
                        fm_group(ps, 128, lambda c, h=h: wk3[:, c, h * 128:(h + 1) * 128], lambda c, hT=hT: hT[:, c, :], 16, [wk, hT])
                        ks = kst[kc % 3]
                        kc += 1
                        eng = evac_engine()
                        if eng == "act":
                            k.op("act", lambda e, ks=ks, ps=ps: e.activation(out=ks[:], in_=ps[:, 0:TT], func=AF.Copy),
                                 reads=[ps], writes=[ks])
                        else:
                            k.op("dve", lambda e, ks=ks, ps=ps: e.tensor_copy(out=ks[:], in_=ps[:, 0:TT]), reads=[ps], writes=[ks])
                        k.dma("pool", lambda e, ks=ks, h=h, j=j: e.dma_start(out=qsd[h, :, j * TT:(j + 1) * TT], in_=ks[:]),
                              reads=[ks], writes=[qsd], key="sc_q")
            else:
                self.stage1_mla(k, G, P, B, stg, cnt, xa, xo, tpb, nacc, evac_engine, fm_group)
            k.emit()
            self.stats["s1" + which] = k.n_ops

    def rstd_from_ps(self, k, ps, dst, n, width):
        k.op("act", lambda e: e.activation(out=dst[:, 0:width], in_=ps[:, 0:width], func=AF.Ln, scale=1.0 / n, bias=EPS),
             reads=[ps], writes=[dst])
        k.op("act", lambda e: e.activation(out=dst[:, 0:width], in_=dst[:, 0:width], func=AF.Exp, scale=-0.5),
             reads=[dst], writes=[dst])

    def stage1_mla(self, k, G, P, B, stg, cnt, xa, xo, tpb, nacc, evac_engine, fm_group):
        cfg, I, Sc = self.cfg, self.I, self.Sc
        TT, NB = cfg.TT, cfg.NB
        hT = B["hT"]
        cv = G["cv"]
        ones = G["ones"]
        wc = k.sb("wc", [128, 16 * 512], BF16)
        wpe = k.sb("wpe", [128, 16 * 256], BF16)
        wun = k.sb("wun", [128, 4 * 1024], BF16)
        wuv = k.sb("wuv", [128, 4 * 1024], BF16)
        wuq = k.sb("wuq", [128, 4 * 3072], BF16)
        self.load_resident(k, self.dr(k, "w_ckv", I["w_ckv"]), wc, 16 * 512, stg, cnt)
        self.load_resident(k, self.dr(k, "w_kpe", I["w_kpe"]), wpe, 16 * 256, stg, cnt)
        self.load_resident(k, self.dr(k, "w_ukn", I["w_ukn"]), wun, 4 * 1024, stg, cnt)
        self.load_resident(k, self.dr(k, "w_ukv", I["w_ukv"]), wuv, 4 * 1024, stg, cnt)
        wc3 = wc[:].rearrange("p (c n) -> p c n", c=16)
        wpe3 = wpe[:].rearrange("p (c n) -> p c n", c=16)
        wun3 = wun[:].rearrange("p (c n) -> p c n", c=4)
        wuv3 = wuv[:].rearrange("p (c n) -> p c n", c=4)
        wuq3 = wuq[:].rearrange("p (c n) -> p c n", c=4)
        T = self.rope_bufs(k)
        cT = k.sb("cT", [128, 4, TT], F32)
        sq = k.sb("sq", [128, 4, TT], BF16)
        R = k.sb("R", [128, TT], F32)
        cn = k.sb("cn", [128, 4, TT], BF16)
        pe32 = k.sb("pe32", [64, TT], F32)
        pe32b = k.sb("pe32b", [64, TT], F32)
        rk = k.sb("rk", [64, TT], F32)
        sqpe = k.sb("sqpe", [128, TT], BF16)
        sqh = [k.sb("sqh%d" % i, [128, TT], BF16) for i in range(2)]
        Rh = [k.sb("Rh%d" % i, [128, TT], F32) for i in range(2)]
        kst = [k.sb("kst%d" % i, [128, TT], BF16) for i in range(3)]
        pst = [k.sb("pst%d" % i, [128, TT], BF16) for i in range(3)]
        for pp_ in pst:
            k.op("dve", lambda e, pp_=pp_: e.memset(pp_[:], 0.0), writes=[pp_])
        vt = [k.sb("vt%d" % i, [128, NB, 1024], BF16) for i in range(2)]
        pa, po = self.dr(k, "pa", I["pa"]), self.dr(k, "po", I["po"])
        knd, kpd, vmd, qnd, qpd = (self.dr(k, n, Sc[n]) for n in ("kn", "kp", "vm", "qn", "qp"))
        psA, psB = P[6], P[7]
        cnts = {"k": 0, "p": 0}

        def latent(gname):
            for nb in range(4):
                ps = nacc()
                fm_group(ps, 128, lambda c, nb=nb: wc3[:, c, nb * 128:(nb + 1) * 128], lambda c: hT[:, c, :], 16, [wc, hT])
                k.op("act", lambda e, ps=ps, nb=nb: e.activation(out=cT[:, nb, :], in_=ps[:, 0:TT], func=AF.Copy),
                     reads=[ps], writes=[cT])
                k.op("act", lambda e, ps=ps, nb=nb: e.activation(out=sq[:, nb, :], in_=ps[:, 0:TT], func=AF.Square),
                     reads=[ps], writes=[sq])
            ps = nacc()
            fm_group(ps, 128, lambda c: ones[:, :], lambda c: sq[:, c, :], 4, [ones, sq])
            self.rstd_from_ps(k, ps, R, 512, TT)
            for nb in range(4):
                k.op("dve", lambda e, nb=nb: e.scalar_tensor_tensor(
                    out=cn[:, nb, :], in0=cT[:, nb, :], scalar=cvs(cv, gname, nb), in1=R[:, 0:TT],
                    op0=ALU.mult, op1=ALU.mult), reads=[cT, cv, R], writes=[cn])

        def rope_apply(ps_a, ps_b, g_a, g_b):
            k.op("dve", lambda e: e.scalar_tensor_tensor(out=pe32[:], in0=ps_a[0:64, 0:TT], scalar=cvs(cv, g_a)[0:64, :],
                                                         in1=T["C2"][:], op0=ALU.mult, op1=ALU.mult),
                 reads=[ps_a, cv, T["C2"]], writes=[pe32])
            k.op("dve", lambda e: e.scalar_tensor_tensor(out=pe32b[:], in0=ps_b[0:64, 0:TT], scalar=cvs(cv, g_b)[0:64, :],
                                                         in1=T["S2"][:], op0=ALU.mult, op1=ALU.mult),
                 reads=[ps_b, cv, T["S2"]], writes=[pe32b])
            k.op("dve", lambda e: e.tensor_tensor(out=rk[:], in0=pe32[:], in1=pe32b[:], op=ALU.add),
                 reads=[pe32, pe32b], writes=[rk])

        cut = getattr(cfg, "cut", 99)
        for kt in range(getattr(cfg, "ntk", cfg.NT)):
            self.prologue(k, G, xa, kt * TT, B, tpb)
            if cut < 1:
                return
            self.rope_tables(k, G, pa, kt * TT, T)
            if cut < 2:
                return
            latent("gkv")
            if cut < 3:
                return
            fm_group(psA, 128, lambda c: wpe3[:, c, 0:128], lambda c: hT[:, c, :], 16, [wpe, hT])
            fm_group(psB, 128, lambda c: wpe3[:, c, 128:256], lambda c: hT[:, c, :], 16, [wpe, hT])
            k.op("act", lambda e: e.activation(out=sqpe[:], in_=psA[:, 0:TT], func=AF.Square, scale=math.sqrt(0.5)), reads=[psA], writes=[sqpe])
            rope_apply(psA, psB, "gkpe", "gkpes")
            if cut < 4:
                return
            for h in range(H):
                if cut < 5 and h > 0:
                    return
                ps = nacc()
                fm_group(ps, 128, lambda c, h=h: wun3[:, c, h * 128:(h + 1) * 128], lambda c: cn[:, c, :], 4, [wun, cn])
                sh, rh = sqh[h % 2], Rh[h % 2]
                k.op("act", lambda e, ps=ps, sh=sh: e.activation(out=sh[:], in_=ps[:, 0:TT], func=AF.Square), reads=[ps], writes=[sh])
                ps2 = nacc()
                k.op("pe", lambda e, ps2=ps2, sh=sh: e.matmul(out=ps2[:, 0:TT], lhsT=ones[:, :], rhs=sh[:], start=True, stop=False),
                     reads=[ones, sh], writes=[ps2])
                k.op("pe", lambda e, ps2=ps2: e.matmul(out=ps2[:, 0:TT], lhsT=ones[:, :], rhs=sqpe[:], start=False, stop=True),
                     reads=[ones, sqpe], writes=[ps2])
                self.rstd_from_ps(k, ps2, rh, 192, TT)
                ks = kst[cnts["k"] % 3]
                cnts["k"] += 1
                k.op("dve", lambda e, ks=ks, ps=ps, rh=rh: e.scalar_tensor_tensor(
                    out=ks[:], in0=ps[:, 0:TT], scalar=cvs(cv, "gkh"), in1=rh[:, 0:TT], op0=ALU.mult, op1=ALU.mult),
                    reads=[ps, cv, rh], writes=[ks])
                k.dma("pool", lambda e, ks=ks, h=h, kt=kt: e.dma_start(out=knd[h, :, kt * TT:(kt + 1) * TT], in_=ks[:]),
                      reads=[ks], writes=[knd], key="sc_k")
                pp = pst[cnts["p"] % 3]
                cnts["p"] += 1
                k.op("dve", lambda e, pp=pp, rh=rh: e.tensor_tensor(out=pp[0:64, :], in0=rk[:], in1=rh[0:64, 0:TT], op=ALU.mult),
                     reads=[rk, rh], writes=[pp])
                k.dma("pool", lambda e, pp=pp, h=h, kt=kt: e.dma_start(out=kpd[h, :, kt * TT:(kt + 1) * TT], in_=pp[:]),
                      reads=[pp], writes=[kpd], key="sc_p")
            if cut < 6:
                return
            v = vt[kt % 2]
            for blk in range(NB):
                for nt in range(2):
                    ps = nacc()
                    for c in range(4):
                        k.op("pe", lambda e, c=c, blk=blk, nt=nt, ps=ps: e.matmul(
                            out=ps[:, 0:512], lhsT=cn[:, c, blk * 128:(blk + 1) * 128],
                            rhs=wuv3[:, c, nt * 512:(nt + 1) * 512], start=(c == 0), stop=(c == 3)),
                            reads=[cn, wuv], writes=[ps])
                    eng = evac_engine()
                    if eng == "act":
                        k.op("act", lambda e, v=v, ps=ps, blk=blk, nt=nt: e.activation(
                            out=v[:, blk, nt * 512:(nt + 1) * 512], in_=ps[:, 0:512], func=AF.Copy), reads=[ps], writes=[v])
                    else:
                        k.op("dve", lambda e, v=v, ps=ps, blk=blk, nt=nt: e.tensor_copy(
                            out=v[:, blk, nt * 512:(nt + 1) * 512], in_=ps[:, 0:512]), reads=[ps], writes=[v])
            for h in range(H):
                k.dma("pool", lambda e, v=v, h=h, kt=kt: e.dma_start(
                    out=vmd[h, :, kt * NB:(kt + 1) * NB, :], in_=v[:, :, h * 128:(h + 1) * 128]),
                    reads=[v], writes=[vmd], key="sc_v")
        if cut < 7:
            return
        self.load_resident(k, self.dr(k, "w_cq", I["w_cq"]), wc, 16 * 512, stg, cnt)
        self.load_resident(k, self.dr(k, "w_uq", I["w_uq"]), wuq, 4 * 3072, stg, cnt)
        if cut < 9:
            return
        for j in range(cfg.NO):
            self.prologue(k, G, xo, j * TT, B, tpb)
            self.rope_tables(k, G, po, j * TT, T)
            latent("gq")
            if cut < 10:
                return
            for h in range(H):
                if cut < 20 and h > 0:
                    return
                ps = nacc()
                fm_group(ps, 128, lambda c, h=h: wuq3[:, c, h * 384:h * 384 + 128], lambda c: cn[:, c, :], 4, [wuq, cn])
                fm_group(psA, 128, lambda c, h=h: wuq3[:, c, h * 384 + 128:h * 384 + 256], lambda c: cn[:, c, :], 4, [wuq, cn])
                fm_group(psB, 128, lambda c, h=h: wuq3[:, c, h * 384 + 256:h * 384 + 384], lambda c: cn[:, c, :], 4, [wuq, cn])
                if cut < 11:
                    return
                sh, rh = sqh[h % 2], Rh[h % 2]
                k.op("act", lambda e, ps=ps, sh=sh: e.activation(out=sh[:], in_=ps[:, 0:TT], func=AF.Square), reads=[ps], writes=[sh])
                if cut < 11.2:
                    return
                k.op("act", lambda e: e.activation(out=sqpe[:], in_=psA[:, 0:TT], func=AF.Square, scale=math.sqrt(0.5)), reads=[psA], writes=[sqpe])
                if cut < 11.4:
                    return
                ps2 = nacc()
                var = getattr(cfg, "var", 0)
                if var == 5:
                    dmy = k.sb("dmy%d_%d" % (j, h), [128, 8], F32)
                    k.op("act", lambda e, dmy=dmy: e.activation(out=dmy[:], in_=G["idf"][:, 0:8], func=AF.Copy), reads=[G["idf"]], writes=[dmy])
                if var == 1:
                    k.op("pe", lambda e, ps2=ps2, sh=sh: e.matmul(out=ps2[:, 0:TT], lhsT=ones[:, :], rhs=sh[:], start=True, stop=True),
                         reads=[ones, sh], writes=[ps2])
                elif var == 2:
                    k.op("pe", lambda e, ps2=ps2, sh=sh: e.matmul(out=ps2[:, 0:TT], lhsT=ones[:, :], rhs=sh[:], start=True, stop=False),
                         reads=[ones, sh], writes=[ps2])
                    k.op("pe", lambda e, ps2=ps2, sh=sh: e.matmul(out=ps2[:, 0:TT], lhsT=ones[:, :], rhs=sh[:], start=False, stop=True),
                         reads=[ones, sh], writes=[ps2])
                else:
                    k.op("pe", lambda e, ps2=ps2, sh=sh: e.matmul(out=ps2[:, 0:TT], lhsT=ones[:, :], rhs=sh[:], start=True, stop=False),
                         reads=[ones, sh], writes=[ps2])
                    k.op("pe", lambda e, ps2=ps2: e.matmul(out=ps2[:, 0:TT], lhsT=ones[:, :], rhs=sqpe[:], start=False, stop=True),
                         reads=[ones, sqpe], writes=[ps2])
                if cut < 11.6:
                    if cfg.debug:
                        dbt = k.sb("dbt", [128, 3, TT], F32)
                        k.op("dve", lambda e, ps2=ps2: e.tensor_copy(out=dbt[:, 0, :], in_=ps2[:, 0:TT]), reads=[ps2], writes=[dbt])
                        k.op("dve", lambda e, sh=sh: e.tensor_copy(out=dbt[:, 1, :], in_=sh[:]), reads=[sh], writes=[dbt])
                        k.op("dve", lambda e: e.tensor_copy(out=dbt[:, 2, :], in_=sqpe[:]), reads=[sqpe], writes=[dbt])
                        dd = self.dr(k, "s_dbg", Sc["dbg"])
                        k.dma("pool", lambda e: e.dma_start(out=dd[:], in_=dbt[:]), reads=[dbt], writes=[dd], key="dbg")
                    return
                if True:
                    k.op("dve", lambda e, ps2=ps2, rh=rh: e.tensor_copy(out=rh[:, 0:TT], in_=ps2[:, 0:TT]), reads=[ps2], writes=[rh])
                    self.rstd_from_ps(k, rh, rh, 192, TT)
                else:
                    self.rstd_from_ps(k, ps2, rh, 192, TT)
                if cut < 12:
                    return
                rope_apply(psA, psB, "gqpe", "gqpes")
                if cut < 13:
                    return
                ks = kst[cnts["k"] % 3]
                cnts["k"] += 1
                k.op("dve", lambda e, ks=ks, ps=ps, rh=rh: e.scalar_tensor_tensor(
                    out=ks[:], in0=ps[:, 0:TT], scalar=cvs(cv, "gqh"), in1=rh[:, 0:TT], op0=ALU.mult, op1=ALU.mult),
                    reads=[ps, cv, rh], writes=[ks])
                k.dma("pool", lambda e, ks=ks, h=h, j=j: e.dma_start(out=qnd[h, :, j * TT:(j + 1) * TT], in_=ks[:]),
                      reads=[ks], writes=[qnd], key="sc_q")
                pp = pst[cnts["p"] % 3]
                cnts["p"] += 1
                k.op("dve", lambda e, pp=pp, rh=rh: e.tensor_tensor(out=pp[0:64, :], in0=rk[:], in1=rh[0:64, 0:TT], op=ALU.mult),
                     reads=[rk, rh], writes=[pp])
                k.dma("pool", lambda e, pp=pp, h=h, j=j: e.dma_start(out=qpd[h, :, j * TT:(j + 1) * TT], in_=pp[:]),
                      reads=[pp], writes=[qpd], key="sc_qp")

    def blocks(self):
        cfg = self.cfg
        NB = cfg.NB
        out = []
        for h in range(H):
            for j in range(cfg.NO):
                nkb = cfg.nkt[j] * NB
                for i, kb in enumerate(reversed(range(nkb))):
                    di = kb - (nkb - 2 * NB)
                    out.append(dict(h=h, j=j, kb=kb, first=(i == 0), last=(i == nkb - 1),
                                    di=(di if di >= 0 else None), n=len(out)))
        return out

    def stage2_sb(self):
        cfg, I, Sc = self.cfg, self.I, self.Sc
        TT, NB, S, SO, NBLK = cfg.TT, cfg.NB, cfg.S, cfg.SO, cfg.NBLK
        with ExitStack() as st:
            k, G, P = self.newk(st)
            ones, tri = G["ones"], G["tri"]
            mk = k.sb("mk", [128, 2 * 2 * NB * TT], BF16)
            mkd = self.dr(k, "mks", I["mks"])
            k.dma("sp", lambda e: e.dma_start(out=mk[:], in_=mkd[:]), reads=[mkd], writes=[mk])
            mk4 = mk[:].rearrange("p (s b t) -> p s b t", s=2, b=2 * NB)
            K1 = [k.sb("K1_%d" % i, [128, S], BF16) for i in range(2)]
            K2 = [k.sb("K2_%d" % i, [128, S], BF16) for i in range(2)]
            V = [k.sb("V_%d" % i, [128, NBLK, 128], BF16) for i in range(2)]
            Q = [k.sb("Q_%d" % i, [128, SO], BF16) for i in range(2)]
            e_t = [k.sb("e%d" % i, [128, TT], F32) for i in range(2)]
            l_t = [k.sb("l%d" % i, [128, TT], F32) for i in range(2)]
            M_t = [k.sb("M%d" % i, [128, TT], BF16) for i in range(4)]
            t3_t = [k.sb("t3%d" % i, [128, TT], F32) for i in range(2)]
            a_t = [k.sb("a%d" % i, [128, TT], BF16) for i in range(4)]
            carry = k.sb("carry", [128, TT], F32)
            yo = [k.sb("yo%d" % i, [128, TT], BF16) for i in range(2)]
            z_ps = [P[0], P[1], P[2]]
            t2_ps = [P[3], P[4]]
            cs_ps = P[5]
            y_ps = [P[6], P[7]]
            ksd, vsd, qsd, ybd = (self.dr(k, n, Sc[n]) for n in ("ksb", "vsb", "qsb", "yb"))

            def load_head(h):
                b = h % 2
                k.dma("sp", lambda e: e.dma_start(out=K1[b][:], in_=ksd[h]), reads=[ksd], writes=[K1[b]])
                k.dma("sp", lambda e: e.dma_start(out=V[b][:], in_=vsd[h]), reads=[vsd], writes=[V[b]])
                k.dma("sp", lambda e: e.dma_start(out=Q[b][:], in_=qsd[h]), reads=[qsd], writes=[Q[b]])
                k.op("pool", lambda e: e.tensor_scalar(out=K2[b][:], in0=K1[b][:], scalar1=-1.0, scalar2=None, op0=ALU.mult),
                     reads=[K1[b]], writes=[K2[b]])

            blks = self.blocks()

            def stA(bk):
                n, h, j, kb = bk["n"], bk["h"], bk["j"], bk["kb"]
                b = h % 2
                z, et, lt, Mt = z_ps[n % 3], e_t[n % 2], l_t[n % 2], M_t[n % 4]
                k.op("pe", lambda e: e.matmul(out=z[:, 0:TT], lhsT=K1[b][:, kb * 128:(kb + 1) * 128],
                                              rhs=Q[b][:, j * TT:(j + 1) * TT], start=True, stop=True),
                     reads=[K1[b], Q[b]], writes=[z])
                k.op("act", lambda e: e.activation(out=et[:], in_=z[:, 0:TT], func=AF.Exp, scale=-1.0), reads=[z], writes=[et])
                k.op("act", lambda e: e.activation(out=lt[:], in_=et[:], func=AF.Ln, bias=1.0), reads=[et], writes=[lt])
                k.op("dve", lambda e: e.tensor_tensor(out=Mt[:], in0=z[:, 0:TT], in1=lt[:], op=ALU.add), reads=[z, lt], writes=[Mt])
                if bk["di"] is not None:
                    di, s = bk["di"], j % 2
                    k.op("pool", lambda e: e.tensor_tensor(out=Mt[:], in0=Mt[:], in1=mk4[:, s, di, :], op=ALU.mult),
                         reads=[Mt, mk], writes=[Mt])

            def stB(bk):
                n, h, j, kb = bk["n"], bk["h"], bk["j"], bk["kb"]
                b = h % 2
                Mt, t2, t3, at = M_t[n % 4], t2_ps[n % 2], t3_t[n % 2], a_t[n % 4]
                k.op("pe", lambda e: e.matmul(out=t2[:, 0:TT], lhsT=tri[:, :], rhs=Mt[:], start=True, stop=False),
                     reads=[tri, Mt], writes=[t2])
                k.op("pe", lambda e: e.matmul(out=t2[:, 0:TT], lhsT=K2[b][:, kb * 128:(kb + 1) * 128],
                                              rhs=Q[b][:, j * TT:(j + 1) * TT], start=False, stop=True),
                     reads=[K2[b], Q[b]], writes=[t2])
                if not bk["last"]:
                    k.op("pe", lambda e: e.matmul(out=cs_ps[:, 0:TT], lhsT=ones[:, :], rhs=Mt[:], start=True, stop=True),
                         reads=[ones, Mt], writes=[cs_ps])
                if bk["first"]:
                    k.op("act", lambda e: e.activation(out=at[:], in_=t2[:, 0:TT], func=AF.Exp, scale=-1.0), reads=[t2], writes=[at])
                    if not bk["last"]:
                        k.op("dve", lambda e: e.tensor_copy(out=carry[:], in_=cs_ps[:, 0:TT]), reads=[cs_ps], writes=[carry])
                else:
                    k.op("dve", lambda e: e.tensor_tensor(out=t3[:], in0=t2[:, 0:TT], in1=carry[:], op=ALU.add),
                         reads=[t2, carry], writes=[t3])
                    k.op("act", lambda e: e.activation(out=at[:], in_=t3[:], func=AF.Exp, scale=-1.0), reads=[t3], writes=[at])
                    if not bk["last"]:
                        k.op("dve", lambda e: e.tensor_tensor(out=carry[:], in0=cs_ps[:, 0:TT], in1=carry[:], op=ALU.add),
                             reads=[cs_ps, carry], writes=[carry])
                if bk["di"] is not None:
                    di, s = bk["di"], j % 2
                    k.op("pool", lambda e: e.tensor_tensor(out=at[:], in0=at[:], in1=mk4[:, s, di, :], op=ALU.mult),
                         reads=[at, mk], writes=[at])

            def stC(bk):
                n, h, j, kb = bk["n"], bk["h"], bk["j"], bk["kb"]
                b = h % 2
                at = a_t[n % 4]
                yp = y_ps[(h * cfg.NO + j) % 2]
                k.op("pe", lambda e: e.matmul(out=yp[:, 0:TT], lhsT=V[b][:, kb, :], rhs=at[:], start=bk["first"], stop=bk["last"]),
                     reads=[V[b], at], writes=[yp])
                if bk["last"]:
                    y = yo[(h * cfg.NO + j) % 2]
                    k.op("dve", lambda e: e.tensor_copy(out=y[:], in_=yp[:, 0:TT]), reads=[yp], writes=[y])
                    k.dma("pool", lambda e: e.dma_start(out=ybd[h, :, j * TT:(j + 1) * TT], in_=y[:]),
                          reads=[y], writes=[ybd], key="sc_y")

            n = len(blks)
            load_head(0)
            load_head(1)
            for it in range(n + 4):
                if it < n:
                    stA(blks[it])
                if 0 <= it - 2 < n:
                    stB(blks[it - 2])
                if 0 <= it - 4 < n:
                    stC(blks[it - 4])
                    bk = blks[it - 4]
                    if bk["first"] and bk["j"] == 0 and 1 <= bk["h"] < H - 1:
                        load_head(bk["h"] + 1)
            k.emit()
            self.stats["s2sb"] = k.n_ops

    def stage2_mla(self):
        cfg, I, Sc = self.cfg, self.I, self.Sc
        TT, NB, S, SO, NBLK = cfg.TT, cfg.NB, cfg.S, cfg.SO, cfg.NBLK
        with ExitStack() as st:
            k, G, P = self.newk(st)
            ones = G["ones"]
            mk = k.sb("mk", [128, 2 * 2 * NB * TT], BF16)
            mkd = self.dr(k, "mki", I["mki"])
            k.dma("sp", lambda e: e.dma_start(out=mk[:], in_=mkd[:]), reads=[mkd], writes=[mk])
            mk4 = mk[:].rearrange("p (s b t) -> p s b t", s=2, b=2 * NB)
            Kn = [k.sb("Kn_%d" % i, [128, S], BF16) for i in range(2)]
            Kp = [k.sb("Kp_%d" % i, [128, S], BF16) for i in range(2)]
            V = [k.sb("V_%d" % i, [128, NBLK, 128], BF16) for i in range(2)]
            Qn = [k.sb("Qn_%d" % i, [128, SO], BF16) for i in range(2)]
            Qp = [k.sb("Qp_%d" % i, [128, SO], BF16) for i in range(2)]
            e_t = [k.sb("e%d" % i, [128, TT], BF16) for i in range(4)]
            rden = k.sb("rden", [128, TT], F32)
            yo = [k.sb("yo%d" % i, [128, TT], BF16) for i in range(2)]
            s_ps = [P[0], P[1], P[2]]
            y_ps = [P[3], P[4]]
            d_ps = [P[5], P[6]]
            knd, kpd, vmd, qnd, qpd, yad = (self.dr(k, n, Sc[n]) for n in ("kn", "kp", "vm", "qn", "qp", "ya"))

            def load_head(h):
                b = h % 2
                k.dma("sp", lambda e: e.dma_start(out=Kn[b][:], in_=knd[h]), reads=[knd], writes=[Kn[b]])
                k.dma("sp", lambda e: e.dma_start(out=Kp[b][:], in_=kpd[h]), reads=[kpd], writes=[Kp[b]])
                k.dma("sp", lambda e: e.dma_start(out=V[b][:], in_=vmd[h]), reads=[vmd], writes=[V[b]])
                k.dma("sp", lambda e: e.dma_start(out=Qn[b][:], in_=qnd[h]), reads=[qnd], writes=[Qn[b]])
                k.dma("sp", lambda e: e.dma_start(out=Qp[b][:], in_=qpd[h]), reads=[qpd], writes=[Qp[b]])

            blks = self.blocks()

            def stA(bk):
                n, h, j, kb = bk["n"], bk["h"], bk["j"], bk["kb"]
                b = h % 2
                sp_, et = s_ps[n % 3], e_t[n % 4]
                k.op("pe", lambda e: e.matmul(out=sp_[:, 0:TT], lhsT=Kn[b][:, kb * 128:(kb + 1) * 128],
                                              rhs=Qn[b][:, j * TT:(j + 1) * TT], start=True, stop=False),
                     reads=[Kn[b], Qn[b]], writes=[sp_])
                k.op("pe", lambda e: e.matmul(out=sp_[:, 0:TT], lhsT=Kp[b][:, kb * 128:(kb + 1) * 128],
                                              rhs=Qp[b][:, j * TT:(j + 1) * TT], start=False, stop=True),
                     reads=[Kp[b], Qp[b]], writes=[sp_])
                k.op("act", lambda e: e.activation(out=et[:], in_=sp_[:, 0:TT], func=AF.Exp, scale=SC_MLA), reads=[sp_], writes=[et])
                if bk["di"] is not None:
                    di, s = bk["di"], j % 2
                    k.op("pool", lambda e: e.tensor_tensor(out=et[:], in0=et[:], in1=mk4[:, s, di, :], op=ALU.mult),
                         reads=[et, mk], writes=[et])

            def stB(bk):
                n, h, j, kb = bk["n"], bk["h"], bk["j"], bk["kb"]
                b = h % 2
                et = e_t[n % 4]
                par = (h * cfg.NO + j) % 2
                yp, dp = y_ps[par], d_ps[par]
                k.op("pe", lambda e: e.matmul(out=yp[:, 0:TT], lhsT=V[b][:, kb, :], rhs=et[:], start=bk["first"], stop=bk["last"]),
                     reads=[V[b], et], writes=[yp])
                k.op("pe", lambda e: e.matmul(out=dp[:, 0:TT], lhsT=ones[:, :], rhs=et[:], start=bk["first"], stop=bk["last"]),
                     reads=[ones, et], writes=[dp])
                if bk["last"]:
                    y = yo[par]
                    k.op("dve", lambda e: e.reciprocal(out=rden[:], in_=dp[:, 0:TT]), reads=[dp], writes=[rden])
                    k.op("dve", lambda e: e.tensor_tensor(out=y[:], in0=yp[:, 0:TT], in1=rden[:], op=ALU.mult),
                         reads=[yp, rden], writes=[y])
                    k.dma("pool", lambda e: e.dma_start(out=yad[h, :, j * TT:(j + 1) * TT], in_=y[:]),
                          reads=[y], writes=[yad], key="sc_y")

            n = len(blks)
            load_head(0)
            load_head(1)
            for it in range(n + 2):
                if it < n:
                    stA(blks[it])
                if 0 <= it - 2 < n:
                    stB(blks[it - 2])
                    bk = blks[it - 2]
                    if bk["first"] and bk["j"] == 0 and 1 <= bk["h"] < H - 1:
                        load_head(bk["h"] + 1)
            k.emit()
            self.stats["s2mla"] = k.n_ops

    def stage3(self):
        cfg, I, Sc = self.cfg, self.I, self.Sc
        TT, NB, SO = cfg.TT, cfg.NB, cfg.SO
        with ExitStack() as st:
            k, G, P = self.newk(st)
            gc, ones = G["gc"], G["ones"]
            need = 2 * D * 4 + NB * D * 2 + D * 2
            big = k.sb("big", [128, max(NFF * TT, need // 2)], BF16)
            bigf = big[:].bitcast(F32)
            B = {}
            o = 0
            xb0 = k.wrap("xblk0", bigf[:, o:o + D], "sb"); o += D
            xb1 = k.wrap("xblk1", bigf[:, o:o + D], "sb"); o += D
            B["xblk"] = [xb0, xb1]
            ob = 2 * o
            xnv = k.wrap("xn", big[:, ob:ob + NB * D].rearrange("p (b f) -> p b f", b=NB), "sb"); ob += NB * D
            junk = k.wrap("junk", big[:, ob:ob + D], "sb"); ob += D
            B["xn"], B["junk"] = xnv, junk
            GT = k.wrap("GT", big[:, 0:NFF * TT].rearrange("p (c t) -> p c t", c=NFF), "sb")
            k.alias(GT, xb0, xb1, xnv, junk)
            for b_ in (xb0, xb1, xnv, junk):
                b_.aliases = [GT]
            GT.aliases = [xb0, xb1, xnv, junk]
            B["ss"] = k.sb("ss", [128, NB], F32)
            B["rs"] = k.sb("rs", [128, NB], F32)
            B["hT"] = k.sb("hT", [128, 16, TT], BF16)
            hT = B["hT"]
            xT = k.sb("xT", [128, 16, TT], F32)
            reg = k.sb("reg", [128, 32 * TT], BF16)
            Ya = k.wrap("Ya", reg[:, 0:8 * TT].rearrange("p (h t) -> p h t", h=8), "sb")
            Yb = k.wrap("Yb", reg[:, 8 * TT:16 * TT].rearrange("p (h t) -> p h t", h=8), "sb")
            mg = k.wrap("mg", reg[:, 16 * TT:32 * TT].rearrange("p (c t) -> p c t", c=16), "sb")
            regf = reg[:].bitcast(F32)
            x2t = k.wrap("x2t", regf[:, 0:4 * TT].rearrange("p (c t) -> p c t", c=4), "sb")
            ost = [k.wrap("ost%d" % i, regf[:, (4 + 4 * i) * TT // 1:(8 + 4 * i) * TT // 1].rearrange("p (b f) -> p b f", b=NB), "sb")
                   for i in range(2)]
            ph1, ph2 = [Ya, Yb, mg], [x2t] + ost
            for b_ in ph1:
                b_.aliases = list(ph2)
            for b_ in ph2:
                b_.aliases = list(ph1)
            sa = [k.sb("sa%d" % i, [128, TT], F32) for i in range(2)]
            sb_ = [k.sb("sb%d" % i, [128, TT], F32) for i in range(2)]
            m1 = [k.sb("m1%d" % i, [128, TT], F32) for i in range(2)]
            m2 = [k.sb("m2%d" % i, [128, TT], F32) for i in range(2)]
            sq2 = [k.sb("sq2%d" % i, [128, TT], BF16) for i in range(2)]
            R2 = k.sb("R2", [128, TT], F32)
            tmp = [k.sb("tmp%d" % i, [128, TT], F32) for i in range(2)]
            sg = [k.sb("sg%d" % i, [128, TT], F32) for i in range(2)]
            NW = 3
            wst = [k.sb("wst%d" % i, [128, 2048], F32) for i in range(NW)]
            wbf = [k.sb("wbf%d" % i, [128, 2048], BF16) for i in range(NW)]
            wc = [0]
            xo = self.dr(k, "xo", I["xo"])
            outd = self.dr(k, "out", self.out)
            yad, ybd = self.dr(k, "ya", Sc["ya"]), self.dr(k, "yb", Sc["yb"])
            W = {n: self.dr(k, n, I[n]) for n in ("w_ga", "w_gb", "w_pa", "w_pb", "w_o", "w_f1", "w_f2")}
            tpb = [P[0], P[1]]
            tpf = [P[2], P[3]]
            accs = [P[2], P[3], P[4], P[5], P[6], P[7]]
            ai = [0]

            def nacc():
                ai[0] += 1
                return accs[ai[0] % len(accs)]

            woff = {"w_ga": 0, "w_gb": 16, "w_pa": 32, "w_pb": 48, "w_o": 64, "w_f1": 80, "w_f2": 168}
            wsc = self.dr(k, "s_wbf", self.nc.dram_tensor("s_wbf", [232, 128, 2048], BF16, kind="Internal").ap())
            cur_tile = [0]

            def wunit(name, u, ncols):
                i = wc[0] % NW
                wc[0] += 1
                s, b = wst[i], wbf[i]
                wd = W[name]
                gu = woff[name] + u
                if cur_tile[0] > 0:
                    k.dma("sp", lambda e: e.dma_start(out=b[:, 0:ncols], in_=wsc[gu, :, 0:ncols]), reads=[wsc], writes=[b])
                    return b
                k.dma("sp", lambda e: e.dma_start(out=s[:, 0:ncols], in_=wd[u]), reads=[wd], writes=[s])
                eng = ("dve", "pool", "act")[wc[0] % 3]
                if eng == "act":
                    k.op("act", lambda e: e.activation(out=b[:, 0:ncols], in_=s[:, 0:ncols], func=AF.Copy), reads=[s], writes=[b])
                else:
                    k.op(eng, lambda e: e.tensor_copy(out=b[:, 0:ncols], in_=s[:, 0:ncols]), reads=[s], writes=[b])
                k.dma("pool", lambda e: e.dma_start(out=wsc[gu, :, 0:ncols], in_=b[:, 0:ncols]), reads=[b], writes=[wsc], key="wsc")
                return b

            def group(ps, wb, nch, rhs_fn, rbufs, start=True, stop=True):
                for c in range(nch):
                    k.op("pe", lambda e, c=c: e.matmul(out=ps[:, 0:TT], lhsT=wb[:, c * 128:(c + 1) * 128], rhs=rhs_fn(c),
                                                       start=(start and c == 0), stop=(stop and c == nch - 1)),
                         reads=[wb] + rbufs, writes=[ps])

            for j in range(cfg.NO):
                t0 = j * TT
                cur_tile[0] = j
                self.prologue(k, G, xo, t0, B, tpb, f32T=xT, tpf=tpf)
                k.dma("sp", lambda e, t0=t0: e.dma_start(out=Ya[:], in_=yad[:, :, t0:t0 + TT].rearrange("h p t -> p h t")),
                      reads=[yad], writes=[Ya])
                k.dma("sp", lambda e, t0=t0: e.dma_start(out=Yb[:], in_=ybd[:, :, t0:t0 + TT].rearrange("h p t -> p h t")),
                      reads=[ybd], writes=[Yb])
                for nb in range(16):
                    i2 = nb % 2
                    wga = wunit("w_ga", nb, 2048)
                    pga = nacc()
                    group(pga, wga, 16, lambda c: hT[:, c, :], [hT])
                    k.op("act", lambda e, pga=pga, i2=i2: e.activation(out=sa[i2][:], in_=pga[:, 0:TT], func=AF.Sigmoid),
                         reads=[pga], writes=[sa[i2]])
                    wgb = wunit("w_gb", nb, 2048)
                    pgb = nacc()
                    group(pgb, wgb, 16, lambda c: hT[:, c, :], [hT])
                    k.op("act", lambda e, pgb=pgb, i2=i2: e.activation(out=sb_[i2][:], in_=pgb[:, 0:TT], func=AF.Sigmoid),
                         reads=[pgb], writes=[sb_[i2]])
                    wpa = wunit("w_pa", nb, 1024)
                    ppa = nacc()
                    group(ppa, wpa, 8, lambda c: Ya[:, c, :], [Ya])
                    k.op("dve", lambda e, ppa=ppa, i2=i2: e.tensor_tensor(out=m1[i2][:], in0=ppa[:, 0:TT], in1=sa[i2][:], op=ALU.mult),
                         reads=[ppa, sa[i2]], writes=[m1[i2]])
                    wpb = wunit("w_pb", nb, 1024)
                    ppb = nacc()
                    group(ppb, wpb, 8, lambda c: Yb[:, c, :], [Yb])
                    k.op("dve", lambda e, ppb=ppb, i2=i2: e.tensor_tensor(out=m2[i2][:], in0=ppb[:, 0:TT], in1=sb_[i2][:], op=ALU.mult),
                         reads=[ppb, sb_[i2]], writes=[m2[i2]])
                    k.op("pool", lambda e, nb=nb, i2=i2: e.tensor_tensor(out=mg[:, nb, :], in0=m1[i2][:], in1=m2[i2][:], op=ALU.add),
                         reads=[m1[i2], m2[i2]], writes=[mg])
                if cfg.debug:
                    dd = self.dr(k, "s_dmg" + str(j), Sc["dmg"])
                    k.dma("pool", lambda e, dd=dd, j=j: e.dma_start(out=dd[j], in_=mg[:]), reads=[mg], writes=[dd], key="dbg")
                pss = P[1]
                for mb in range(16):
                    wo = wunit("w_o", mb, 2048)
                    po_ = nacc()
                    group(po_, wo, 16, lambda c: mg[:, c, :], [mg])
                    k.op("dve", lambda e, po_=po_, mb=mb: e.scalar_tensor_tensor(
                        out=xT[:, mb, :], in0=po_[:, 0:TT], scalar=gc[:, 32 + mb:33 + mb], in1=xT[:, mb, :],
                        op0=ALU.mult, op1=ALU.add), reads=[po_, gc, xT], writes=[xT])
                    s2 = sq2[mb % 2]
                    k.op("act", lambda e, s2=s2, mb=mb: e.activation(out=s2[:], in_=xT[:, mb, :], func=AF.Square),
                         reads=[xT], writes=[s2])
                    k.op("pe", lambda e, s2=s2, mb=mb: e.matmul(out=pss[:, 0:TT], lhsT=ones[:, :], rhs=s2[:],
                                                                start=(mb == 0), stop=(mb == 15)),
                         reads=[ones, s2], writes=[pss])
                self.rstd_from_ps(k, pss, R2, D, TT)
                for mb in range(16):
                    t_ = tmp[mb % 2]
                    k.op("dve", lambda e, t_=t_, mb=mb: e.scalar_tensor_tensor(
                        out=t_[:], in0=xT[:, mb, :], scalar=gc[:, 48 + mb:49 + mb], in1=R2[:], op0=ALU.mult, op1=ALU.mult),
                        reads=[xT, gc, R2], writes=[t_])
                    k.op("act", lambda e, t_=t_, mb=mb: e.activation(out=hT[:, mb, :], in_=t_[:], func=AF.Identity,
                                                                     bias=gc[:, 64 + mb:65 + mb]),
                         reads=[t_, gc], writes=[hT])
                for fb in range(NFF):
                    wg = wunit("w_f1", fb, 2048)
                    pg = nacc()
                    group(pg, wg, 16, lambda c: hT[:, c, :], [hT])
                    s_ = sg[fb % 2]
                    k.op("act", lambda e, pg=pg, s_=s_: e.activation(out=s_[:], in_=pg[:, 0:TT], func=AF.Silu), reads=[pg], writes=[s_])
                    wu = wunit("w_f1", NFF + fb, 2048)
                    pu = nacc()
                    group(pu, wu, 16, lambda c: hT[:, c, :], [hT])
                    k.op("dve", lambda e, pu=pu, s_=s_, fb=fb: e.tensor_tensor(out=GT[:, fb, :], in0=pu[:, 0:TT], in1=s_[:], op=ALU.mult),
                         reads=[pu, s_], writes=[GT])
                if cfg.debug:
                    for nm, src in (("dx1", xT), ("dG", GT), ("dh2", hT)):
                        dd = self.dr(k, "s_" + nm + str(j), Sc[nm])
                        k.dma("pool", lambda e, dd=dd, src=src, j=j: e.dma_start(out=dd[j], in_=src[:]),
                              reads=[src], writes=[dd], key="dbg")
                for mg4 in range(4):
                    for q in range(4):
                        mb = mg4 * 4 + q
                        pf = nacc()
                        for part in range(4):
                            w2 = wunit("w_f2", mb * 4 + part, 11 * 128)
                            group(pf, w2, 11, lambda c, part=part: GT[:, part * 11 + c, :], [GT],
                                  start=(part == 0), stop=(part == 3))
                        k.op("dve", lambda e, pf=pf, mb=mb, q=q: e.scalar_tensor_tensor(
                            out=x2t[:, q, :], in0=pf[:, 0:TT], scalar=gc[:, 80 + mb:81 + mb], in1=xT[:, mb, :],
                            op0=ALU.mult, op1=ALU.add), reads=[pf, gc, xT], writes=[x2t])
                    os_ = ost[mg4 % 2]
                    for blk in range(NB):
                        pt = tpf[blk % 2]
                        for q in range(4):
                            k.op("pe", lambda e, pt=pt, q=q, blk=blk: e.transpose(
                                out=pt[:, q * 128:(q + 1) * 128], in_=x2t[:, q, blk * 128:(blk + 1) * 128], identity=G["idf"][:]),
                                reads=[x2t, G["idf"]], writes=[pt])
                        k.op("act", lambda e, pt=pt, os_=os_, blk=blk: e.activation(out=os_[:, blk, :], in_=pt[:, 0:512], func=AF.Copy),
                             reads=[pt], writes=[os_])
                    k.dma("pool", lambda e, os_=os_, mg4=mg4, t0=t0: e.dma_start(
                        out=outd[t0:t0 + TT, mg4 * 512:(mg4 + 1) * 512].rearrange("(b p) f -> p b f", p=128), in_=os_[:]),
                        reads=[os_], writes=[outd], key="out")
            k.emit()
            self.stats["s3"] = k.n_ops


def lay(w):
    Kd, N = w.shape
    C = Kd // 128
    return np.ascontiguousarray(w.reshape(C, 128, N).transpose(1, 0, 2).reshape(128, C * N))


def lay_units(w, ncol=128):
    Kd, N = w.shape
    C = Kd // 128
    U = N // ncol
    return np.ascontiguousarray(w.reshape(C, 128, U, ncol).transpose(2, 1, 0, 3).reshape(U, 128, C * ncol))


def fm(v):
    return np.ascontiguousarray(v.reshape(-1, 128).T)


def make_masks(cfg, half, strict):
    TT, NB = cfg.TT, cfg.NB
    out = np.zeros((128, 2, 2 * NB, TT), np.float32)
    p = np.arange(128)[:, None, None]
    b = np.arange(2 * NB)[None, :, None]
    t = np.arange(TT)[None, None, :]
    key = b * 128 + p
    for s in range(2):
        qoff = TT * ((s == 1) if half == 0 else (s == 0))
        qry = qoff + t
        out[:, s] = (key < qry) if strict else (key <= qry)
    return out.reshape(128, -1).astype(ml_dtypes.bfloat16)


def prep_shared(cfg, inp):
    f32 = np.float32
    w_in = np.asarray(inp["w_in"][0], f32)
    offs = np.cumsum([0, 512, 512, 64, 1024, 1024, 1024, 2048, 2048])
    c_q, c_kv, k_pe, q_sb, k_sb, v_sb, gl_a, gl_b = [w_in[:, offs[i]:offs[i + 1]] for i in range(8)]
    Wd = {}
    Wd["w_ada"] = lay_units(np.asarray(inp["w_ada"][0], f32))
    Wd["w_ksb"] = lay(k_sb)
    Wd["w_vsb"] = lay(v_sb)
    Wd["w_qsb"] = lay(q_sb)
    Wd["w_ckv"] = lay(c_kv)
    kpe_sw = np.concatenate([k_pe[:, 32:64], k_pe[:, 0:32]], axis=1)
    Wd["w_kpe"] = lay(np.concatenate([k_pe, kpe_sw, kpe_sw, k_pe], axis=1))
    Wd["w_cq"] = lay(c_q)
    w_ukv = np.asarray(inp["w_ukv"][0], f32).reshape(512, H, 256)
    Wd["w_ukn"] = lay(np.ascontiguousarray(w_ukv[:, :, 0:128]).reshape(512, 1024))
    Wd["w_ukv"] = lay(np.ascontiguousarray(w_ukv[:, :, 128:256]).reshape(512, 1024))
    w_uq = np.asarray(inp["w_uq"][0], f32).reshape(512, H, 192)
    pe_, pesw_ = w_uq[:, :, 128:192], np.concatenate([w_uq[:, :, 160:192], w_uq[:, :, 128:160]], axis=2)
    uq = np.concatenate([w_uq[:, :, 0:128], pe_, pesw_, pesw_, pe_], axis=2)
    Wd["w_uq"] = lay(np.ascontiguousarray(uq).reshape(512, 3072))
    Wd["w_ga"] = lay_units(gl_a)
    Wd["w_gb"] = lay_units(gl_b)
    Wd["w_pa"] = lay_units(np.asarray(inp["w_proj_mla"][0], f32))
    Wd["w_pb"] = lay_units(np.asarray(inp["w_proj_sb"][0], f32))
    Wd["w_o"] = lay_units(np.asarray(inp["w_out"][0], f32))
    Wd["w_f1"] = lay_units(np.asarray(inp["w_ffn_in"][0], f32))
    w2 = np.asarray(inp["w_ffn_out"][0], f32)
    w2r = w2.reshape(4, 11, 128, 16, 128).transpose(3, 0, 2, 1, 4).reshape(64, 128, 11 * 128)
    Wd["w_f2"] = np.ascontiguousarray(w2r)
    Wd["idf"] = np.eye(128, dtype=f32)
    jj = np.arange(128)[:, None]
    ss = np.arange(128)[None, :]
    Wd["tri"] = (jj >= ss).astype(ml_dtypes.bfloat16)
    cvb = np.zeros((128, NCV), f32)

    def put(name, arr):
        a, b = CV[name]
        cvb[:arr.shape[0], a:b] = arr
    put("bada", fm(np.asarray(inp["b_ada"][0], f32)))
    put("g1", fm(np.asarray(inp["g_norm1"][0], f32)))
    put("g2", fm(np.asarray(inp["g_norm2"][0], f32)))
    put("gq", fm(np.asarray(inp["g_q_latent"][0], f32)))
    put("gkv", fm(np.asarray(inp["g_kv_latent"][0], f32)))
    gqh = np.asarray(inp["g_q_head"][0], f32)
    gkh = np.asarray(inp["g_k_head"][0], f32)
    put("gqh", gqh[0:128, None])
    put("gkh", gkh[0:128, None])
    put("gqpe", gqh[128:192, None])
    put("gqpes", np.concatenate([gqh[160:192], gqh[128:160]])[:, None])
    put("gkpe", gkh[128:192, None])
    put("gkpes", np.concatenate([gkh[160:192], gkh[128:160]])[:, None])
    fr = (np.float32(10000.0) ** (-np.arange(32, dtype=f32) / np.float32(32))).astype(f32)
    put("freq", np.concatenate([fr, fr])[:, None])
    Wd["_cv"] = cvb
    return Wd


def prep_core(cfg, inp, Wd, core):
    f32 = np.float32
    b, half = core // 2, core % 2
    TT = cfg.TT
    x = np.asarray(inp["x"][b], f32)
    pos = np.asarray(inp["positions"][b], np.int32)
    own = cfg.own[half]
    rows = np.concatenate([np.arange(t * TT, (t + 1) * TT) for t in own])
    cvb = Wd["_cv"].copy()
    a, e = CV["c"]
    cvb[:, a:e] = fm(np.asarray(inp["c"][b], f32))
    m = {k_: v for k_, v in Wd.items() if not k_.startswith("_")}
    m["xa"] = x
    m["xo"] = np.ascontiguousarray(x[rows])
    m["pa"] = np.ascontiguousarray(np.broadcast_to(pos[None, :], (64, pos.shape[0])))
    m["po"] = np.ascontiguousarray(np.broadcast_to(pos[rows][None, :], (64, rows.shape[0])))
    m["cv"] = cvb
    m["mks"] = make_masks(cfg, half, True)
    m["mki"] = make_masks(cfg, half, False)
    return m, rows


_CACHE = {}


def run(cfg, inputs):
    key = (cfg.S, cfg.TT, cfg.debug, getattr(cfg, "stages", "012345"))
    if key not in _CACHE:
        p = Prog(cfg)
        p.build()
        _CACHE[key] = p
    p = _CACHE[key]
    Wd = prep_shared(cfg, inputs)
    in_maps, rows_all = [], []
    for core in range(8):
        m, rows = prep_core(cfg, inputs, Wd, core)
        in_maps.append(m)
        rows_all.append(rows)
    used = set(p.I.keys())
    in_maps = [{k_: v for k_, v in m.items() if k_ in used} for m in in_maps]
    res = run_bass_kernel_spmd(p.nc, in_maps, core_ids=list(range(8)))
    return res, rows_all


def kernel(**inputs):
    cfg = Cfg()
    res, rows_all = run(cfg, inputs)
    B_ = inputs["x"].shape[0]
    out = np.zeros((B_, cfg.S, D), np.float32)
    for core in range(8):
        out[core // 2, rows_all[core]] = res.results[core]["out"]
    return out
```

```python
from contextlib import ExitStack
import math
import numpy as np
import ml_dtypes
import concourse.bass as bass
import concourse.mybir as mybir
from concourse.bass_utils import run_bass_kernel_spmd

F32 = mybir.dt.float32
BF16 = mybir.dt.bfloat16
I32 = mybir.dt.int32
ALU = mybir.AluOpType
AF = mybir.ActivationFunctionType

D = 2048
H = 8
DFF = 5632
NC16 = 16
NFF = 44
EPS = 1e-6
SC_SB = 128 ** -0.5
SC_MLA = 192 ** -0.5
TWO_PI = 2.0 * math.pi
CW1 = 6.28125
CW2 = TWO_PI - CW1


class Buf:
    def __init__(self, name, t, space):
        self.name, self.t, self.space = name, t, space
        self.last_w = None
        self.readers = []
        self.aliases = []

    def __getitem__(self, idx):
        return self.t[idx]


class Op:
    __slots__ = ("eng", "fn", "waits", "idx", "inc", "dkey")

    def __init__(self, eng, fn, waits, idx, dkey=None):
        self.eng, self.fn, self.waits, self.idx, self.dkey = eng, fn, waits, idx, dkey
        self.inc = False


class SemPool:
    ENGS = ("pe", "act", "dve", "pool", "sp")

    def __init__(self, nc, stack):
        self.nc, self.stack = nc, stack
        self.esem = {e: stack.enter_context(nc.semaphore("tl_" + e)) for e in self.ENGS}
        self.ebase = {e: 0 for e in self.ENGS}
        self.dsem = []
        self.dbase = []
        self.free = {"hw": [], "sw": []}

    def dslot(self, kind, i):
        lst = self.free[kind]
        while len(lst) <= i:
            lst.append(len(self.dsem))
            self.dsem.append(self.stack.enter_context(self.nc.semaphore("dq%s%d" % (kind, len(self.dsem)))))
            self.dbase.append(0)
        return lst[i]


class K:
    ENGS = SemPool.ENGS

    _n = [0]

    def __init__(self, nc, stack, pool):
        self.nc, self.stack, self.pool = nc, stack, pool
        K._n[0] += 1
        self.pfx = "k%d_" % K._n[0]
        self.ops = {e: [] for e in self.ENGS}
        self.dcount = {}
        self.dkeys = []
        self.dkind = {}

    def sb(self, name, shape, dtype):
        return Buf(name, self.stack.enter_context(self.nc.sbuf_tensor(self.pfx + name, list(shape), dtype)), "sb")

    def wrap(self, name, t, space):
        return Buf(name, t, space)

    def alias(self, *bufs):
        for b in bufs:
            b.aliases = [o for o in bufs if o is not b]

    def _collect(self, eng, reads, writes):
        ev = []
        for b in reads:
            if b.last_w is not None:
                ev.append(b.last_w)
        for b in writes:
            for bb in [b] + b.aliases:
                if bb.last_w is not None:
                    ev.append(bb.last_w)
                ev.extend(bb.readers)
        out, seen = [], set()
        for e in ev:
            if e[0] == "d":
                e = ("d", e[1], self.dcount[e[1]])
            if e[0] == "e" and e[1] == eng and eng in ("pe", "sp"):
                continue
            if e in seen:
                continue
            seen.add(e)
            out.append(e)
        return out

    def _commit(self, ev, reads, writes):
        for b in reads:
            b.readers.append(ev)
        for b in writes:
            b.last_w = ev
            b.readers = []

    def op(self, eng, fn, reads=(), writes=()):
        waits = self._collect(eng, reads, writes)
        idx = len(self.ops[eng])
        self.ops[eng].append(Op(eng, fn, waits, idx))
        self._commit(("e", eng, idx), reads, writes)

    def dma(self, q, fn, reads=(), writes=(), key=None):
        if key is None:
            cands = [b for b in list(writes) + list(reads) if b.space != "dram"]
            key = (cands[0] if cands else (list(writes) + list(reads))[0]).name
        waits = self._collect(q, reads, writes)
        idx = len(self.ops[q])
        kind = "sw" if q == "pool" else "hw"
        if kind == "sw":
            key = "swA" if key in ("sc_k", "sc_q", "sc_y", "out") else "swB"
        if key not in self.dcount:
            self.dcount[key] = 0
            self.dkeys.append(key)
            self.dkind[key] = kind
        assert self.dkind[key] == kind, key
        self.dcount[key] += 1
        self.ops[q].append(Op(q, fn, waits, idx, dkey=key))
        self._commit(("d", key, self.dcount[key]), reads, writes)

    def emit(self):
        nc, pool = self.nc, self.pool
        for e in self.ENGS:
            for o in self.ops[e]:
                for w in o.waits:
                    if w[0] == "e":
                        self.ops[w[1]][w[2]].inc = True
        semval = {}
        newbase = {}
        for e in self.ENGS:
            c = pool.ebase[e]
            for o in self.ops[e]:
                if o.inc:
                    c += 1
                    semval[(e, o.idx)] = c
            newbase[e] = c
        slot = {}
        nk = {"hw": 0, "sw": 0}
        for k in self.dkeys:
            slot[k] = pool.dslot(self.dkind[k], nk[self.dkind[k]])
            nk[self.dkind[k]] += 1
        dbase = {k: pool.dbase[slot[k]] for k in self.dkeys}
        esem, dsem = pool.esem, pool.dsem
        block = self.stack.enter_context(nc.Block())

        def make(ename):
            ops = self.ops[ename]

            def body(eng):
                waited = {}
                for o in ops:
                    for w in o.waits:
                        if w[0] == "e":
                            sem, val, kk = esem[w[1]], semval[(w[1], w[2])], ("e", w[1])
                        else:
                            sem, val, kk = dsem[slot[w[1]]], dbase[w[1]] + 16 * w[2], ("d", w[1])
                        if waited.get(kk, 0) >= val:
                            continue
                        waited[kk] = val
                        eng.wait_ge(sem, val)
                    ins = o.fn(eng)
                    if o.dkey is not None:
                        ins.then_inc(dsem[slot[o.dkey]], 16)
                    elif o.inc:
                        ins.then_inc(esem[ename], 1)
                if ename == "sp":
                    for k in self.dkeys:
                        eng.wait_ge(dsem[slot[k]], dbase[k] + 16 * self.dcount[k])
            return body

        block.sync(make("sp"))
        block.tensor(make("pe"))
        block.scalar(make("act"))
        block.vector(make("dve"))
        block.gpsimd(make("pool"))
        for e in self.ENGS:
            pool.ebase[e] = newbase[e]
        for k in self.dkeys:
            pool.dbase[slot[k]] += 16 * self.dcount[k]
        self.n_ops = {e: len(self.ops[e]) for e in self.ENGS}


class Cfg:
    def __init__(self, S=4096, TT=512, debug=False):
        self.S, self.TT, self.debug = S, TT, debug
        self.NB = TT // 128
        self.NT = S // TT
        assert self.NT == 8
        self.NO = 4
        self.SO = S // 2
        self.NBLK = S // 128
        self.own = {0: [0, 3, 4, 7], 1: [1, 2, 5, 6]}
        self.nkt = [2, 4, 6, 8]


CV = {}
_o = 0
for _n, _w in [("c", 16), ("bada", 96), ("g1", 16), ("g2", 16), ("gq", 4), ("gkv", 4),
               ("gqh", 1), ("gkh", 1), ("gqpe", 1), ("gqpes", 1), ("gkpe", 1), ("gkpes", 1), ("freq", 1)]:
    CV[_n] = (_o, _o + _w)
    _o += _w
NCV = _o


def cvs(cv, name, i=None):
    a, b = CV[name]
    if i is None:
        return cv[:, a:b]
    return cv[:, a + i:a + i + 1]


class Prog:
    def __init__(self, cfg):
        self.cfg = cfg
        self.nc = bass.Bass("TRN2", target_bir_lowering=False)
        self.stats = {}

    def din(self, name, shape, dtype):
        return self.nc.dram_tensor(name, list(shape), dtype, kind="ExternalInput").ap()

    def dscr(self, name, shape, dtype):
        kind = "ExternalOutput" if self.cfg.debug else "Internal"
        return self.nc.dram_tensor(name, list(shape), dtype, kind=kind).ap()

    def build(self):
        cfg, nc = self.cfg, self.nc
        S, TT, NB, SO, NBLK = cfg.S, cfg.TT, cfg.NB, cfg.SO, cfg.NBLK
        specs = {"xa": ([S, D], F32), "xo": ([SO, D], F32), "pa": ([64, S], I32), "po": ([64, SO], I32),
                 "cv": ([128, NCV], F32), "idf": ([128, 128], F32), "tri": ([128, 128], BF16),
                 "mks": ([128, 2 * 2 * NB * TT], BF16), "mki": ([128, 2 * 2 * NB * TT], BF16),
                 "w_ada": ([96, 128, 2048], F32)}
        for n, x in [("w_ksb", 16 * 1024), ("w_vsb", 16 * 1024), ("w_qsb", 16 * 1024), ("w_ckv", 16 * 512),
                     ("w_kpe", 16 * 256), ("w_cq", 16 * 512), ("w_ukn", 4 * 1024), ("w_ukv", 4 * 1024),
                     ("w_uq", 4 * 3072)]:
            specs[n] = ([128, x], F32)
        for n, shp in [("w_ga", [16, 128, 2048]), ("w_gb", [16, 128, 2048]), ("w_pa", [16, 128, 1024]),
                       ("w_pb", [16, 128, 1024]), ("w_o", [16, 128, 2048]), ("w_f1", [88, 128, 2048]),
                       ("w_f2", [64, 128, 11 * 128])]:
            specs[n] = (shp, F32)
        prog = self

        class Lazy(dict):
            def __missing__(self, name):
                shp, dt = specs[name]
                self[name] = prog.din(name, shp, dt)
                return self[name]
        I = Lazy()
        self.I = I
        self.out = nc.dram_tensor("out", [SO, D], F32, kind="ExternalOutput").ap()
        Sc = {}
        Sc["ksb"] = self.dscr("s_ksb", [H, 128, S], BF16)
        Sc["vsb"] = self.dscr("s_vsb", [H, 128, NBLK, 128], BF16)
        Sc["qsb"] = self.dscr("s_qsb", [H, 128, SO], BF16)
        Sc["kn"] = self.dscr("s_kn", [H, 128, S], BF16)
        Sc["kp"] = self.dscr("s_kp", [H, 128, S], BF16)
        Sc["vm"] = self.dscr("s_vm", [H, 128, NBLK, 128], BF16)
        Sc["qn"] = self.dscr("s_qn", [H, 128, SO], BF16)
        Sc["qp"] = self.dscr("s_qp", [H, 128, SO], BF16)
        Sc["ya"] = self.dscr("s_ya", [H, 128, SO], BF16)
        Sc["yb"] = self.dscr("s_yb", [H, 128, SO], BF16)
        if cfg.debug:
            Sc["gc"] = self.dscr("s_gc", [128, 96], F32)
            Sc["dbg"] = self.dscr("s_dbg", [128, 3, TT], F32)
            Sc["dx1"] = self.dscr("s_dx1", [4, 128, 16 * TT], F32)
            Sc["dG"] = self.dscr("s_dG", [4, 128, NFF * TT], BF16)
            Sc["dh2"] = self.dscr("s_dh2", [4, 128, 16 * TT], BF16)
            Sc["dmg"] = self.dscr("s_dmg", [4, 128, 16 * TT], BF16)
        self.Sc = Sc

        with ExitStack() as gst:
            self.gst = gst
            self.pool = SemPool(nc, gst)
            self.psum = [gst.enter_context(nc.psum_tensor("ps%d" % i, [128, 512], F32)) for i in range(8)]
            g = {}
            g["cv"] = gst.enter_context(nc.sbuf_tensor("g_cv", [128, NCV], F32))
            g["gc"] = gst.enter_context(nc.sbuf_tensor("g_gc", [128, 96], F32))
            g["idf"] = gst.enter_context(nc.sbuf_tensor("g_idf", [128, 128], F32))
            g["idb"] = gst.enter_context(nc.sbuf_tensor("g_idb", [128, 128], BF16))
            g["ones"] = gst.enter_context(nc.sbuf_tensor("g_ones", [128, 128], BF16))
            g["tri"] = gst.enter_context(nc.sbuf_tensor("g_tri", [128, 128], BF16))
            self.g = g
            stages = getattr(cfg, "stages", "012345")
            self.stage0(ada=("0" in stages))
            if "1" in stages:
                self.stage1("sb")
            if "2" in stages:
                self.stage2_sb()
            if "3" in stages:
                self.stage1("mla")
            if "4" in stages:
                self.stage2_mla()
            if "5" in stages:
                self.stage3()
        return nc

    def newk(self, st):
        k = K(self.nc, st, self.pool)
        G = {n: k.wrap(n, t, "sb") for n, t in self.g.items()}
        P = [k.wrap("ps%d" % i, t, "ps") for i, t in enumerate(self.psum)]
        return k, G, P

    def dr(self, k, name, ap):
        return k.wrap(name, ap, "dram")

    def stage0(self, ada=True):
        nc, I = self.nc, self.I
        with ExitStack() as st:
            k, G, P = self.newk(st)
            cvd, idfd, trid = (self.dr(k, n, I[n]) for n in ("cv", "idf", "tri"))
            wad = self.dr(k, "w_ada", I["w_ada"]) if ada else None
            k.dma("sp", lambda e: e.dma_start(out=G["cv"][:], in_=cvd[:]), reads=[cvd], writes=[G["cv"]])
            k.dma("sp", lambda e: e.dma_start(out=G["idf"][:], in_=idfd[:]), reads=[idfd], writes=[G["idf"]])
            k.dma("sp", lambda e: e.dma_start(out=G["tri"][:], in_=trid[:]), reads=[trid], writes=[G["tri"]])
            k.op("dve", lambda e: e.tensor_copy(out=G["idb"][:], in_=G["idf"][:]), reads=[G["idf"]], writes=[G["idb"]])
            k.op("dve", lambda e: e.memset(G["ones"][:], 1.0), writes=[G["ones"]])
            if not ada:
                k.op("dve", lambda e: e.memset(G["gc"][:], 1.0), writes=[G["gc"]])
                k.emit()
                return
            cact = k.sb("cact", [128, 16], F32)
            k.op("act", lambda e: e.activation(out=cact[:], in_=cvs(G["cv"], "c"), func=AF.Silu),
                 reads=[G["cv"]], writes=[cact])
            wst = [k.sb("wst%d" % i, [128, 2048], F32) for i in range(3)]
            ps = P[0]
            for fb in range(96):
                w = wst[fb % 3]
                k.dma("sp", lambda e, w=w, fb=fb: e.dma_start(out=w[:], in_=wad[fb]), reads=[wad], writes=[w])
                for c in range(16):
                    k.op("pe", lambda e, w=w, fb=fb, c=c: e.matmul(
                        out=ps[:, fb:fb + 1], lhsT=w[:, c * 128:(c + 1) * 128], rhs=cact[:, c:c + 1],
                        start=(c == 0), stop=(c == 15)), reads=[w, cact], writes=[ps])
            ada = k.sb("ada", [128, 96], F32)
            k.op("dve", lambda e: e.tensor_tensor(out=ada[:], in0=ps[:, 0:96], in1=cvs(G["cv"], "bada"), op=ALU.add),
                 reads=[ps, G["cv"]], writes=[ada])
            gc = G["gc"]
            k.op("dve", lambda e: e.scalar_tensor_tensor(out=gc[:, 0:16], in0=ada[:, 16:32], scalar=1.0,
                                                         in1=cvs(G["cv"], "g1"), op0=ALU.add, op1=ALU.mult),
                 reads=[ada, G["cv"]], writes=[gc])
            k.op("dve", lambda e: e.scalar_tensor_tensor(out=gc[:, 48:64], in0=ada[:, 64:80], scalar=1.0,
                                                         in1=cvs(G["cv"], "g2"), op0=ALU.add, op1=ALU.mult),
                 reads=[ada, G["cv"]], writes=[gc])
            for (a, b, c) in [(16, 0, 16), (32, 32, 16), (64, 48, 16), (80, 80, 16)]:
                k.op("dve", lambda e, a=a, b=b, c=c: e.tensor_copy(out=gc[:, a:a + c], in_=ada[:, b:b + c]),
                     reads=[ada], writes=[gc])
            if self.cfg.debug:
                gcd = self.dr(k, "s_gc", self.Sc["gc"])
                k.dma("pool", lambda e: e.dma_start(out=gcd[:], in_=gc[:]), reads=[gc], writes=[gcd])
            k.emit()
            self.stats["s0"] = k.n_ops

    def load_resident(self, k, wd, dst, ncols, stg, cnt):
        off = 0
        while off < ncols:
            n = min(2048, ncols - off)
            s = stg[cnt[0] % len(stg)]
            eng = "pool" if cnt[0] % 2 == 0 else "dve"
            cnt[0] += 1
            k.dma("sp", lambda e, s=s, off=off, n=n: e.dma_start(out=s[:, 0:n], in_=wd[:, off:off + n]),
                  reads=[wd], writes=[s])
            k.op(eng, lambda e, s=s, off=off, n=n: e.tensor_copy(out=dst[:, off:off + n], in_=s[:, 0:n]),
                 reads=[s], writes=[dst])
            off += n

    def prologue(self, k, G, xd, row0, B, tpb, f32T=None, tpf=None, gcoff=0):
        TT, NB = self.cfg.TT, self.cfg.NB
        xblk, xn, junk, ss, rs, hT = B["xblk"], B["xn"], B["junk"], B["ss"], B["rs"], B["hT"]
        gc = G["gc"]
        for blk in range(NB):
            xb = xblk[blk % 2]
            r0 = row0 + blk * 128
            k.dma("sp", lambda e, xb=xb, r0=r0: e.dma_start(out=xb[:], in_=xd[r0:r0 + 128, :]), reads=[xd], writes=[xb])
            k.op("act", lambda e, xb=xb, blk=blk: e.activation(out=junk[:], in_=xb[:], func=AF.Square,
                                                               accum_out=ss[:, blk:blk + 1]),
                 reads=[xb], writes=[junk, ss])
            k.op("act", lambda e, blk=blk: e.activation(out=rs[:, blk:blk + 1], in_=ss[:, blk:blk + 1], func=AF.Ln,
                                                        scale=1.0 / D, bias=EPS), reads=[ss], writes=[rs])
            k.op("act", lambda e, blk=blk: e.activation(out=rs[:, blk:blk + 1], in_=rs[:, blk:blk + 1], func=AF.Exp,
                                                        scale=-0.5), reads=[rs], writes=[rs])
            k.op("act", lambda e, xb=xb, blk=blk: e.activation(out=xn[:, blk, :], in_=xb[:], func=AF.Copy,
                                                               scale=rs[:, blk:blk + 1]),
                 reads=[xb, rs], writes=[xn])
            if f32T is not None:
                for cg in range(4):
                    p = tpf[cg % 2]
                    for j in range(4):
                        c = cg * 4 + j
                        k.op("pe", lambda e, p=p, j=j, c=c, xb=xb: e.transpose(
                            out=p[:, j * 128:(j + 1) * 128], in_=xb[:, c * 128:(c + 1) * 128], identity=G["idf"][:]),
                            reads=[xb, G["idf"]], writes=[p])
                    k.op("dve", lambda e, p=p, cg=cg, blk=blk: e.tensor_copy(
                        out=f32T[:, cg * 4:(cg + 1) * 4, blk * 128:(blk + 1) * 128],
                        in_=p[:, 0:512].rearrange("p (j t) -> p j t", j=4)), reads=[p], writes=[f32T])
        for c in range(16):
            p = tpb[c % 2]
            pv = p[:, 0:TT // 2].bitcast(BF16)
            for blk in range(NB):
                k.op("pe", lambda e, pv=pv, c=c, blk=blk: e.transpose(
                    out=pv[:, blk * 128:(blk + 1) * 128], in_=xn[:, blk, c * 128:(c + 1) * 128], identity=G["idb"][:]),
                    reads=[xn, G["idb"]], writes=[p])
            k.op("dve", lambda e, pv=pv, c=c: e.tensor_scalar(
                out=hT[:, c, :], in0=pv[:, 0:TT], scalar1=gc[:, gcoff + c:gcoff + c + 1],
                scalar2=gc[:, gcoff + 16 + c:gcoff + 17 + c], op0=ALU.mult, op1=ALU.add),
                reads=[p, gc], writes=[hT])

    def prologue_bufs(self, k):
        TT, NB = self.cfg.TT, self.cfg.NB
        B = {}
        B["xblk"] = [k.sb("xblk%d" % i, [128, D], F32) for i in range(2)]
        B["xn"] = k.sb("xn", [128, NB, D], BF16)
        B["junk"] = k.sb("junk", [128, D], BF16)
        B["ss"] = k.sb("ss", [128, NB], F32)
        B["rs"] = k.sb("rs", [128, NB], F32)
        B["hT"] = k.sb("hT", [128, 16, TT], BF16)
        return B

    def rope_tables(self, k, G, posd, col0, T):
        TT = self.cfg.TT
        pi_, pf, ang, kk, r, r2, C2, S2 = (T[n] for n in ("pi", "pf", "ang", "kk", "r", "r2", "C2", "S2"))
        cv = G["cv"]
        k.dma("sp", lambda e: e.dma_start(out=pi_[:], in_=posd[:, col0:col0 + TT]),
              reads=[posd], writes=[pi_])
        k.op("dve", lambda e: e.tensor_copy(out=pf[:], in_=pi_[:]), reads=[pi_], writes=[pf])
        k.op("dve", lambda e: e.tensor_scalar(out=ang[:], in0=pf[:], scalar1=cvs(cv, "freq")[0:64, :], scalar2=None,
                                              op0=ALU.mult), reads=[pf, cv], writes=[ang])

        MAGIC = 12582912.0

        def reduce_(src, dst):
            k.op("dve", lambda e: e.tensor_scalar(out=kk[:], in0=src[:], scalar1=1.0 / TWO_PI, scalar2=MAGIC,
                                                  op0=ALU.mult, op1=ALU.add), reads=[src], writes=[kk])
            k.op("dve", lambda e: e.tensor_scalar(out=kk[:], in0=kk[:], scalar1=-MAGIC, scalar2=None,
                                                  op0=ALU.add), reads=[kk], writes=[kk])
            k.op("dve", lambda e: e.scalar_tensor_tensor(out=r[:], in0=kk[:], scalar=-CW1, in1=src[:],
                                                         op0=ALU.mult, op1=ALU.add), reads=[kk, src], writes=[r])
            k.op("dve", lambda e: e.scalar_tensor_tensor(out=r[:], in0=kk[:], scalar=-CW2, in1=r[:],
                                                         op0=ALU.mult, op1=ALU.add), reads=[kk, r], writes=[r])
            k.op("dve", lambda e: e.tensor_scalar(out=dst[:], in0=r[:], scalar1=math.pi, scalar2=-math.pi,
                                                  op0=ALU.min, op1=ALU.max), reads=[r], writes=[dst])

        reduce_(ang, r2)
        k.op("act", lambda e: e.activation(out=S2[:], in_=r2[:], func=AF.Sin), reads=[r2], writes=[S2])
        k.op("dve", lambda e: e.tensor_scalar(out=S2[0:32, :], in0=S2[0:32, :], scalar1=-1.0, scalar2=None,
                                              op0=ALU.mult), reads=[S2], writes=[S2])
        k.op("dve", lambda e: e.tensor_scalar(out=ang[:], in0=r2[:], scalar1=math.pi / 2, scalar2=None,
                                              op0=ALU.add), reads=[r2], writes=[ang])
        reduce_(ang, r2)
        k.op("act", lambda e: e.activation(out=C2[:], in_=r2[:], func=AF.Sin), reads=[r2], writes=[C2])

    def rope_bufs(self, k):
        TT = self.cfg.TT
        T = {}
        T["pi"] = k.sb("r_pi", [64, TT], I32)
        for n in ("pf", "ang", "kk", "r", "r2", "C2", "S2"):
            T[n] = k.sb("r_" + n, [64, TT], F32)
        return T

    def stage1(self, which):
        cfg, I, Sc = self.cfg, self.I, self.Sc
        TT, NB, S = cfg.TT, cfg.NB, cfg.S
        with ExitStack() as st:
            k, G, P = self.newk(st)
            B = self.prologue_bufs(k)
            hT = B["hT"]
            stg = [k.sb("stg%d" % i, [128, 2048], F32) for i in range(3 if which == "sb" else 2)]
            cnt = [0]
            xa, xo = self.dr(k, "xa", I["xa"]), self.dr(k, "xo", I["xo"])
            tpb = [P[0], P[1]]
            accs = [P[2], P[3], P[4], P[5]]
            ai = [0]

            def nacc():
                ai[0] += 1
                return accs[ai[0] % 4]

            ev = [0]

            def evac_engine():
                ev[0] += 1
                return "act" if ev[0] % 2 == 0 else "dve"

            def fm_group(ps, M, lhs_fn, rhs_fn, nch, extra_reads):
                for c in range(nch):
                    k.op("pe", lambda e, c=c: e.matmul(out=ps[0:M, 0:TT], lhsT=lhs_fn(c), rhs=rhs_fn(c),
                                                       start=(c == 0), stop=(c == nch - 1)),
                         reads=extra_reads, writes=[ps])

            if which == "sb":
                wk = k.sb("wk", [128, 16 * 1024], BF16)
                wv = k.sb("wv", [128, 16 * 1024], BF16)
                self.load_resident(k, self.dr(k, "w_ksb", I["w_ksb"]), wk, 16 * 1024, stg, cnt)
                self.load_resident(k, self.dr(k, "w_vsb", I["w_vsb"]), wv, 16 * 1024, stg, cnt)
                kst = [k.sb("kst%d" % i, [128, TT], BF16) for i in range(3)]
                vt = [k.sb("vt%d" % i, [128, NB, 1024], BF16) for i in range(2)]
                ksd, vsd, qsd = (self.dr(k, n, Sc[n]) for n in ("ksb", "vsb", "qsb"))
                wk3 = wk[:].rearrange("p (c n) -> p c n", c=16)
                wv3 = wv[:].rearrange("p (c n) -> p c n", c=16)
                kc = 0
                for kt in range(cfg.NT):
                    self.prologue(k, G, xa, kt * TT, B, tpb)
                    for h in range(H):
                        ps = nacc()
                        fm_group(ps, 128, lambda c, h=h: wk3[:, c, h * 128:(h + 1) * 128], lambda c: hT[:, c, :], 16, [wk, hT])
                        ks = kst[kc % 3]
                        kc += 1
                        eng = evac_engine()
                        if eng == "act":
                            k.op("act", lambda e, ks=ks, ps=ps: e.activation(out=ks[:], in_=ps[:, 0:TT], func=AF.Copy, scale=SC_SB),
                                 reads=[ps], writes=[ks])
                        else:
                            k.op("dve", lambda e, ks=ks, ps=ps: e.tensor_scalar(out=ks[:], in0=ps[:, 0:TT], scalar1=SC_SB, scalar2=None,
                                                                                op0=ALU.mult), reads=[ps], writes=[ks])
                        k.dma("pool", lambda e, ks=ks, h=h, kt=kt: e.dma_start(out=ksd[h, :, kt * TT:(kt + 1) * TT], in_=ks[:]),
                              reads=[ks], writes=[ksd], key="sc_k")
                    v = vt[kt % 2]
                    for blk in range(NB):
                        for nt in range(2):
                            ps = nacc()
                            for c in range(16):
                                k.op("pe", lambda e, c=c, blk=blk, nt=nt, ps=ps: e.matmul(
                                    out=ps[:, 0:512], lhsT=hT[:, c, blk * 128:(blk + 1) * 128],
                                    rhs=wv3[:, c, nt * 512:(nt + 1) * 512], start=(c == 0), stop=(c == 15)),
                                    reads=[hT, wv], writes=[ps])
                            eng = evac_engine()
                            if eng == "act":
                                k.op("act", lambda e, v=v, ps=ps, blk=blk, nt=nt: e.activation(
                                    out=v[:, blk, nt * 512:(nt + 1) * 512], in_=ps[:, 0:512], func=AF.Copy),
                                    reads=[ps], writes=[v])
                            else:
                                k.op("dve", lambda e, v=v, ps=ps, blk=blk, nt=nt: e.tensor_copy(
                                    out=v[:, blk, nt * 512:(nt + 1) * 512], in_=ps[:, 0:512]), reads=[ps], writes=[v])
                    for h in range(H):
                        k.dma("pool", lambda e, v=v, h=h, kt=kt: e.dma_start(
                            out=vsd[h, :, kt * NB:(kt + 1) * NB, :], in_=v[:, :, h * 128:(h + 1) * 128]),
                            reads=[v], writes=[vsd], key="sc_v")
                self.load_resident(k, self.dr(k, "w_qsb", I["w_qsb"]), wk, 16 * 1024, stg, cnt)
                for j in range(cfg.NO):
                    self.prologue(k, G, xo, j * TT, B, tpb)
                    for h in range(H):
                        ps = nacc()
                        fm_group(ps, 128, lambda c, h=h: wk3[:, c, h * 128:(h + 1) * 128], lambda c: hT[:, c, :], 16, [wk, hT])
                        ks = kst[kc % 3]
                        kc += 1
                        eng = evac_engine()
                        if eng == "act":
                            k.op("act", lambda e, ks=ks, ps=ps: e.activation(out=ks[:], in_=ps[:, 0:TT], func=AF.Copy),
                                 reads=[ps], writes=[ks])
                        else:
                            k.op("dve", lambda e, ks=ks, ps=ps: e.tensor_copy(out=ks[:], in_=ps[:, 0:TT]), reads=[ps], writes=[ks])
                        k.dma("pool", lambda e, ks=ks, h=h, j=j: e.dma_start(out=qsd[h, :, j * TT:(j + 1) * TT], in_=ks[:]),
                              reads=[ks], writes=[qsd], key="sc_q")
            else:
                self.stage1_mla(k, G, P, B, stg, cnt, xa, xo, tpb, nacc, evac_engine, fm_group)
            k.emit()
            self.stats["s1" + which] = k.n_ops

    def rstd_from_ps(self, k, ps, dst, n, width):
        k.op("act", lambda e: e.activation(out=dst[:, 0:width], in_=ps[:, 0:width], func=AF.Ln, scale=1.0 / n, bias=EPS),
             reads=[ps], writes=[dst])
        k.op("act", lambda e: e.activation(out=dst[:, 0:width], in_=dst[:, 0:width], func=AF.Exp, scale=-0.5),
             reads=[dst], writes=[dst])

    def stage1_mla(self, k, G, P, B, stg, cnt, xa, xo, tpb, nacc, evac_engine, fm_group):
        cfg, I, Sc = self.cfg, self.I, self.Sc
        TT, NB = cfg.TT, cfg.NB
        hT = B["hT"]
        cv = G["cv"]
        ones = G["ones"]
        wc = k.sb("wc", [128, 16 * 512], BF16)
        wpe = k.sb("wpe", [128, 16 * 256], BF16)
        wun = k.sb("wun", [128, 4 * 1024], BF16)
        wuv = k.sb("wuv", [128, 4 * 1024], BF16)
        wuq = k.sb("wuq", [128, 4 * 3072], BF16)
        self.load_resident(k, self.dr(k, "w_ckv", I["w_ckv"]), wc, 16 * 512, stg, cnt)
        self.load_resident(k, self.dr(k, "w_kpe", I["w_kpe"]), wpe, 16 * 256, stg, cnt)
        self.load_resident(k, self.dr(k, "w_ukn", I["w_ukn"]), wun, 4 * 1024, stg, cnt)
        self.load_resident(k, self.dr(k, "w_ukv", I["w_ukv"]), wuv, 4 * 1024, stg, cnt)
        wc3 = wc[:].rearrange("p (c n) -> p c n", c=16)
        wpe3 = wpe[:].rearrange("p (c n) -> p c n", c=16)
        wun3 = wun[:].rearrange("p (c n) -> p c n", c=4)
        wuv3 = wuv[:].rearrange("p (c n) -> p c n", c=4)
        wuq3 = wuq[:].rearrange("p (c n) -> p c n", c=4)
        T = self.rope_bufs(k)
        cT = k.sb("cT", [128, 4, TT], F32)
        sq = k.sb("sq", [128, 4, TT], BF16)
        R = k.sb("R", [128, TT], F32)
        cn = k.sb("cn", [128, 4, TT], BF16)
        pe32 = k.sb("pe32", [64, TT], F32)
        pe32b = k.sb("pe32b", [64, TT], F32)
        rk = k.sb("rk", [64, TT], F32)
        sqpe = k.sb("sqpe", [128, TT], BF16)
        sqh = [k.sb("sqh%d" % i, [128, TT], BF16) for i in range(2)]
        Rh = [k.sb("Rh%d" % i, [128, TT], F32) for i in range(2)]
        kst = [k.sb("kst%d" % i, [128, TT], BF16) for i in range(3)]
        pst = [k.sb("pst%d" % i, [128, TT], BF16) for i in range(3)]
        for pp_ in pst:
            k.op("dve", lambda e, pp_=pp_: e.memset(pp_[:], 0.0), writes=[pp_])
        vt = [k.sb("vt%d" % i, [128, NB, 1024], BF16) for i in range(2)]
        pa, po = self.dr(k, "pa", I["pa"]), self.dr(k, "po", I["po"])
        knd, kpd, vmd, qnd, qpd = (self.dr(k, n, Sc[n]) for n in ("kn", "kp", "vm", "qn", "qp"))
        psA, psB = P[6], P[7]
        cnts = {"k": 0, "p": 0}

        def latent(gname):
            for nb in range(4):
                ps = nacc()
                fm_group(ps, 128, lambda c, nb=nb: wc3[:, c, nb * 128:(nb + 1) * 128], lambda c: hT[:, c, :], 16, [wc, hT])
                k.op("act", lambda e, ps=ps, nb=nb: e.activation(out=cT[:, nb, :], in_=ps[:, 0:TT], func=AF.Copy),
                     reads=[ps], writes=[cT])
                k.op("act", lambda e, ps=ps, nb=nb: e.activation(out=sq[:, nb, :], in_=ps[:, 0:TT], func=AF.Square),
                     reads=[ps], writes=[sq])
            ps = nacc()
            fm_group(ps, 128, lambda c: ones[:, :], lambda c: sq[:, c, :], 4, [ones, sq])
            self.rstd_from_ps(k, ps, R, 512, TT)
            for nb in range(4):
                k.op("dve", lambda e, nb=nb: e.scalar_tensor_tensor(
                    out=cn[:, nb, :], in0=cT[:, nb, :], scalar=cvs(cv, gname, nb), in1=R[:, 0:TT],
                    op0=ALU.mult, op1=ALU.mult), reads=[cT, cv, R], writes=[cn])

        def rope_apply(ps_a, ps_b, g_a, g_b):
            k.op("dve", lambda e: e.scalar_tensor_tensor(out=pe32[:], in0=ps_a[0:64, 0:TT], scalar=cvs(cv, g_a)[0:64, :],
                                                         in1=T["C2"][:], op0=ALU.mult, op1=ALU.mult),
                 reads=[ps_a, cv, T["C2"]], writes=[pe32])
            k.op("dve", lambda e: e.scalar_tensor_tensor(out=pe32b[:], in0=ps_b[0:64, 0:TT], scalar=cvs(cv, g_b)[0:64, :],
                                                         in1=T["S2"][:], op0=ALU.mult, op1=ALU.mult),
                 reads=[ps_b, cv, T["S2"]], writes=[pe32b])
            k.op("dve", lambda e: e.tensor_tensor(out=rk[:], in0=pe32[:], in1=pe32b[:], op=ALU.add),
                 reads=[pe32, pe32b], writes=[rk])

        cut = getattr(cfg, "cut", 99)
        for kt in range(getattr(cfg, "ntk", cfg.NT)):
            self.prologue(k, G, xa, kt * TT, B, tpb)
            if cut < 1:
                return
            self.rope_tables(k, G, pa, kt * TT, T)
            if cut < 2:
                return
            latent("gkv")
            if cut < 3:
                return
            fm_group(psA, 128, lambda c: wpe3[:, c, 0:128], lambda c: hT[:, c, :], 16, [wpe, hT])
            fm_group(psB, 128, lambda c: wpe3[:, c, 128:256], lambda c: hT[:, c, :], 16, [wpe, hT])
            k.op("act", lambda e: e.activation(out=sqpe[:], in_=psA[:, 0:TT], func=AF.Square, scale=math.sqrt(0.5)), reads=[psA], writes=[sqpe])
            rope_apply(psA, psB, "gkpe", "gkpes")
            if cut < 4:
                return
            for h in range(H):
                if cut < 5 and h > 0:
                    return
                ps = nacc()
                fm_group(ps, 128, lambda c, h=h: wun3[:, c, h * 128:(h + 1) * 128], lambda c: cn[:, c, :], 4, [wun, cn])
                sh, rh = sqh[h % 2], Rh[h % 2]
                k.op("act", lambda e, ps=ps, sh=sh: e.activation(out=sh[:], in_=ps[:, 0:TT], func=AF.Square), reads=[ps], writes=[sh])
                ps2 = nacc()
                k.op("pe", lambda e, ps2=ps2, sh=sh: e.matmul(out=ps2[:, 0:TT], lhsT=ones[:, :], rhs=sh[:], start=True, stop=False),
                     reads=[ones, sh], writes=[ps2])
                k.op("pe", lambda e, ps2=ps2: e.matmul(out=ps2[:, 0:TT], lhsT=ones[:, :], rhs=sqpe[:], start=False, stop=True),
                     reads=[ones, sqpe], writes=[ps2])
                self.rstd_from_ps(k, ps2, rh, 192, TT)
                ks = kst[cnts["k"] % 3]
                cnts["k"] += 1
                k.op("dve", lambda e, ks=ks, ps=ps, rh=rh: e.scalar_tensor_tensor(
                    out=ks[:], in0=ps[:, 0:TT], scalar=cvs(cv, "gkh"), in1=rh[:, 0:TT], op0=ALU.mult, op1=ALU.mult),
                    reads=[ps, cv, rh], writes=[ks])
                k.dma("pool", lambda e, ks=ks, h=h, kt=kt: e.dma_start(out=knd[h, :, kt * TT:(kt + 1) * TT], in_=ks[:]),
                      reads=[ks], writes=[knd], key="sc_k")
                pp = pst[cnts["p"] % 3]
                cnts["p"] += 1
                k.op("dve", lambda e, pp=pp, rh=rh: e.tensor_tensor(out=pp[0:64, :], in0=rk[:], in1=rh[0:64, 0:TT], op=ALU.mult),
                     reads=[rk, rh], writes=[pp])
                k.dma("pool", lambda e, pp=pp, h=h, kt=kt: e.dma_start(out=kpd[h, :, kt * TT:(kt + 1) * TT], in_=pp[:]),
                      reads=[pp], writes=[kpd], key="sc_p")
            if cut < 6:
                return
            v = vt[kt % 2]
            for blk in range(NB):
                for nt in range(2):
                    ps = nacc()
                    for c in range(4):
                        k.op("pe", lambda e, c=c, blk=blk, nt=nt, ps=ps: e.matmul(
                            out=ps[:, 0:512], lhsT=cn[:, c, blk * 128:(blk + 1) * 128],
                            rhs=wuv3[:, c, nt * 512:(nt + 1) * 512], start=(c == 0), stop=(c == 3)),
                            reads=[cn, wuv], writes=[ps])
                    eng = evac_engine()
                    if eng == "act":
                        k.op("act", lambda e, v=v, ps=ps, blk=blk, nt=nt: e.activation(
                            out=v[:, blk, nt * 512:(nt + 1) * 512], in_=ps[:, 0:512], func=AF.Copy), reads=[ps], writes=[v])
                    else:
                        k.op("dve", lambda e, v=v, ps=ps, blk=blk, nt=nt: e.tensor_copy(
                            out=v[:, blk, nt * 512:(nt + 1) * 512], in_=ps[:, 0:512]), reads=[ps], writes=[v])
            for h in range(H):
                k.dma("pool", lambda e, v=v, h=h, kt=kt: e.dma_start(
                    out=vmd[h, :, kt * NB:(kt + 1) * NB, :], in_=v[:, :, h * 128:(h + 1) * 128]),
                    reads=[v], writes=[vmd], key="sc_v")
        if cut < 7:
            return
        self.load_resident(k, self.dr(k, "w_cq", I["w_cq"]), wc, 16 * 512, stg, cnt)
        self.load_resident(k, self.dr(k, "w_uq", I["w_uq"]), wuq, 4 * 3072, stg, cnt)
        if cut < 9:
            return
        for j in range(cfg.NO):
            self.prologue(k, G, xo, j * TT, B, tpb)
            self.rope_tables(k, G, po, j * TT, T)
            latent("gq")
            if cut < 10:
                return
            for h in range(H):
                if cut < 20 and h > 0:
                    return
                ps = nacc()
                fm_group(ps, 128, lambda c, h=h: wuq3[:, c, h * 384:h * 384 + 128], lambda c: cn[:, c, :], 4, [wuq, cn])
                fm_group(psA, 128, lambda c, h=h: wuq3[:, c, h * 384 + 128:h * 384 + 256], lambda c: cn[:, c, :], 4, [wuq, cn])
                fm_group(psB, 128, lambda c, h=h: wuq3[:, c, h * 384 + 256:h * 384 + 384], lambda c: cn[:, c, :], 4, [wuq, cn])
                if cut < 11:
                    return
                sh, rh = sqh[h % 2], Rh[h % 2]
                k.op("act", lambda e, ps=ps, sh=sh: e.activation(out=sh[:], in_=ps[:, 0:TT], func=AF.Square), reads=[ps], writes=[sh])
                if cut < 11.2:
                    return
                k.op("act", lambda e: e.activation(out=sqpe[:], in_=psA[:, 0:TT], func=AF.Square, scale=math.sqrt(0.5)), reads=[psA], writes=[sqpe])
                if cut < 11.4:
                    return
                ps2 = nacc()
                var = getattr(cfg, "var", 0)
                if var == 5:
                    dmy = k.sb("dmy%d_%d" % (j, h), [128, 8], F32)
                    k.op("act", lambda e, dmy=dmy: e.activation(out=dmy[:], in_=G["idf"][:, 0:8], func=AF.Copy), reads=[G["idf"]], writes=[dmy])
                if var == 1:
                    k.op("pe", lambda e, ps2=ps2, sh=sh: e.matmul(out=ps2[:, 0:TT], lhsT=ones[:, :], rhs=sh[:], start=True, stop=True),
                         reads=[ones, sh], writes=[ps2])
                elif var == 2:
                    k.op("pe", lambda e, ps2=ps2, sh=sh: e.matmul(out=ps2[:, 0:TT], lhsT=ones[:, :], rhs=sh[:], start=True, stop=False),
                         reads=[ones, sh], writes=[ps2])
                    k.op("pe", lambda e, ps2=ps2, sh=sh: e.matmul(out=ps2[:, 0:TT], lhsT=ones[:, :], rhs=sh[:], start=False, stop=True),
                         reads=[ones, sh], writes=[ps2])
                else:
                    k.op("pe", lambda e, ps2=ps2, sh=sh: e.matmul(out=ps2[:, 0:TT], lhsT=ones[:, :], rhs=sh[:], start=True, stop=False),
                         reads=[ones, sh], writes=[ps2])
                    k.op("pe", lambda e, ps2=ps2: e.matmul(out=ps2[:, 0:TT], lhsT=ones[:, :], rhs=sqpe[:], start=False, stop=True),
                         reads=[ones, sqpe], writes=[ps2])
                if cut < 11.6:
                    if cfg.debug:
                        dbt = k.sb("dbt", [128, 3, TT], F32)
                        k.op("dve", lambda e, ps2=ps2: e.tensor_copy(out=dbt[:, 0, :], in_=ps2[:, 0:TT]), reads=[ps2], writes=[dbt])
                        k.op("dve", lambda e, sh=sh: e.tensor_copy(out=dbt[:, 1, :], in_=sh[:]), reads=[sh], writes=[dbt])
                        k.op("dve", lambda e: e.tensor_copy(out=dbt[:, 2, :], in_=sqpe[:]), reads=[sqpe], writes=[dbt])
                        dd = self.dr(k, "s_dbg", Sc["dbg"])
                        k.dma("pool", lambda e: e.dma_start(out=dd[:], in_=dbt[:]), reads=[dbt], writes=[dd], key="dbg")
                    return
                if True:
                    k.op("dve", lambda e, ps2=ps2, rh=rh: e.tensor_copy(out=rh[:, 0:TT], in_=ps2[:, 0:TT]), reads=[ps2], writes=[rh])
                    self.rstd_from_ps(k, rh, rh, 192, TT)
                else:
                    self.rstd_from_ps(k, ps2, rh, 192, TT)
                if cut < 12:
                    return
                rope_apply(psA, psB, "gqpe", "gqpes")
                if cut < 13:
                    return
                ks = kst[cnts["k"] % 3]
                cnts["k"] += 1
                k.op("dve", lambda e, ks=ks, ps=ps, rh=rh: e.scalar_tensor_tensor(
                    out=ks[:], in0=ps[:, 0:TT], scalar=cvs(cv, "gqh"), in1=rh[:, 0:TT], op0=ALU.mult, op1=ALU.mult),
                    reads=[ps, cv, rh], writes=[ks])
                k.dma("pool", lambda e, ks=ks, h=h, j=j: e.dma_start(out=qnd[h, :, j * TT:(j + 1) * TT], in_=ks[:]),
                      reads=[ks], writes=[qnd], key="sc_q")
                pp = pst[cnts["p"] % 3]
                cnts["p"] += 1
                k.op("dve", lambda e, pp=pp, rh=rh: e.tensor_tensor(out=pp[0:64, :], in0=rk[:], in1=rh[0:64, 0:TT], op=ALU.mult),
                     reads=[rk, rh], writes=[pp])
                k.dma("pool", lambda e, pp=pp, h=h, j=j: e.dma_start(out=qpd[h, :, j * TT:(j + 1) * TT], in_=pp[:]),
                      reads=[pp], writes=[qpd], key="sc_qp")

    def blocks(self):
        cfg = self.cfg
        NB = cfg.NB
        out = []
        for h in range(H):
            for j in range(cfg.NO):
                nkb = cfg.nkt[j] * NB
                for i, kb in enumerate(reversed(range(nkb))):
                    di = kb - (nkb - 2 * NB)
                    out.append(dict(h=h, j=j, kb=kb, first=(i == 0), last=(i == nkb - 1),
                                    di=(di if di >= 0 else None), n=len(out)))
        return out

    def stage2_sb(self):
        cfg, I, Sc = self.cfg, self.I, self.Sc
        TT, NB, S, SO, NBLK = cfg.TT, cfg.NB, cfg.S, cfg.SO, cfg.NBLK
        with ExitStack() as st:
            k, G, P = self.newk(st)
            ones, tri = G["ones"], G["tri"]
            mk = k.sb("mk", [128, 2 * 2 * NB * TT], BF16)
            mkd = self.dr(k, "mks", I["mks"])
            k.dma("sp", lambda e: e.dma_start(out=mk[:], in_=mkd[:]), reads=[mkd], writes=[mk])
            mk4 = mk[:].rearrange("p (s b t) -> p s b t", s=2, b=2 * NB)
            K1 = [k.sb("K1_%d" % i, [128, S], BF16) for i in range(2)]
            K2 = [k.sb("K2_%d" % i, [128, S], BF16) for i in range(2)]
            V = [k.sb("V_%d" % i, [128, NBLK, 128], BF16) for i in range(2)]
            Q = [k.sb("Q_%d" % i, [128, SO], BF16) for i in range(2)]
            e_t = [k.sb("e%d" % i, [128, TT], F32) for i in range(2)]
            l_t = [k.sb("l%d" % i, [128, TT], F32) for i in range(2)]
            M_t = [k.sb("M%d" % i, [128, TT], BF16) for i in range(4)]
            t3_t = [k.sb("t3%d" % i, [128, TT], F32) for i in range(2)]
            a_t = [k.sb("a%d" % i, [128, TT], BF16) for i in range(4)]
            carry = k.sb("carry", [128, TT], F32)
            yo = [k.sb("yo%d" % i, [128, TT], BF16) for i in range(2)]
            z_ps = [P[0], P[1], P[2]]
            t2_ps = [P[3], P[4]]
            cs_ps = P[5]
            y_ps = [P[6], P[7]]
            ksd, vsd, qsd, ybd = (self.dr(k, n, Sc[n]) for n in ("ksb", "vsb", "qsb", "yb"))

            def load_head(h):
                b = h % 2
                k.dma("sp", lambda e: e.dma_start(out=K1[b][:], in_=ksd[h]), reads=[ksd], writes=[K1[b]])
                k.dma("sp", lambda e: e.dma_start(out=V[b][:], in_=vsd[h]), reads=[vsd], writes=[V[b]])
                k.dma("sp", lambda e: e.dma_start(out=Q[b][:], in_=qsd[h]), reads=[qsd], writes=[Q[b]])
                k.op("pool", lambda e: e.tensor_scalar(out=K2[b][:], in0=K1[b][:], scalar1=-1.0, scalar2=None, op0=ALU.mult),
                     reads=[K1[b]], writes=[K2[b]])

            blks = self.blocks()

            def stA(bk):
                n, h, j, kb = bk["n"], bk["h"], bk["j"], bk["kb"]
                b = h % 2
                z, et, lt, Mt = z_ps[n % 3], e_t[n % 2], l_t[n % 2], M_t[n % 4]
                k.op("pe", lambda e: e.matmul(out=z[:, 0:TT], lhsT=K1[b][:, kb * 128:(kb + 1) * 128],
                                              rhs=Q[b][:, j * TT:(j + 1) * TT], start=True, stop=True),
                     reads=[K1[b], Q[b]], writes=[z])
                k.op("act", lambda e: e.activation(out=et[:], in_=z[:, 0:TT], func=AF.Exp, scale=-1.0), reads=[z], writes=[et])
                k.op("act", lambda e: e.activation(out=lt[:], in_=et[:], func=AF.Ln, bias=1.0), reads=[et], writes=[lt])
                k.op("dve", lambda e: e.tensor_tensor(out=Mt[:], in0=z[:, 0:TT], in1=lt[:], op=ALU.add), reads=[z, lt], writes=[Mt])
                if bk["di"] is not None:
                    di, s = bk["di"], j % 2
                    k.op("pool", lambda e: e.tensor_tensor(out=Mt[:], in0=Mt[:], in1=mk4[:, s, di, :], op=ALU.mult),
                         reads=[Mt, mk], writes=[Mt])

            def stB(bk):
                n, h, j, kb = bk["n"], bk["h"], bk["j"], bk["kb"]
                b = h % 2
                Mt, t2, t3, at = M_t[n % 4], t2_ps[n % 2], t3_t[n % 2], a_t[n % 4]
                k.op("pe", lambda e: e.matmul(out=t2[:, 0:TT], lhsT=tri[:, :], rhs=Mt[:], start=True, stop=False),
                     reads=[tri, Mt], writes=[t2])
                k.op("pe", lambda e: e.matmul(out=t2[:, 0:TT], lhsT=K2[b][:, kb * 128:(kb + 1) * 128],
                                              rhs=Q[b][:, j * TT:(j + 1) * TT], start=False, stop=True),
                     reads=[K2[b], Q[b]], writes=[t2])
                if not bk["last"]:
                    k.op("pe", lambda e: e.matmul(out=cs_ps[:, 0:TT], lhsT=ones[:, :], rhs=Mt[:], start=True, stop=True),
                         reads=[ones, Mt], writes=[cs_ps])
                if bk["first"]:
                    k.op("act", lambda e: e.activation(out=at[:], in_=t2[:, 0:TT], func=AF.Exp, scale=-1.0), reads=[t2], writes=[at])
                    if not bk["last"]:
                        k.op("dve", lambda e: e.tensor_copy(out=carry[:], in_=cs_ps[:, 0:TT]), reads=[cs_ps], writes=[carry])
                else:
                    k.op("dve", lambda e: e.tensor_tensor(out=t3[:], in0=t2[:, 0:TT], in1=carry[:], op=ALU.add),
                         reads=[t2, carry], writes=[t3])
                    k.op("act", lambda e: e.activation(out=at[:], in_=t3[:], func=AF.Exp, scale=-1.0), reads=[t3], writes=[at])
                    if not bk["last"]:
                        k.op("dve", lambda e: e.tensor_tensor(out=carry[:], in0=cs_ps[:, 0:TT], in1=carry[:], op=ALU.add),
                             reads=[cs_ps, carry], writes=[carry])
                if bk["di"] is not None:
                    di, s = bk["di"], j % 2
                    k.op("pool", lambda e: e.tensor_tensor(out=at[:], in0=at[:], in1=mk4[:, s, di, :], op=ALU.mult),
                         reads=[at, mk], writes=[at])

            def stC(bk):
                n, h, j, kb = bk["n"], bk["h"], bk["j"], bk["kb"]
                b = h % 2
                at = a_t[n % 4]
                yp = y_ps[(h * cfg.NO + j) % 2]
                k.op("pe", lambda e: e.matmul(out=yp[:, 0:TT], lhsT=V[b][:, kb, :], rhs=at[:], start=bk["first"], stop=bk["last"]),
                     reads=[V[b], at], writes=[yp])
                if bk["last"]:
                    y = yo[(h * cfg.NO + j) % 2]
                    k.op("dve", lambda e: e.tensor_copy(out=y[:], in_=yp[:, 0:TT]), reads=[yp], writes=[y])
                    k.dma("pool", lambda e: e.dma_start(out=ybd[h, :, j * TT:(j + 1) * TT], in_=y[:]),
                          reads=[y], writes=[ybd], key="sc_y")

            n = len(blks)
            load_head(0)
            load_head(1)
            for it in range(n + 4):
                if it < n:
                    stA(blks[it])
                if 0 <= it - 2 < n:
                    stB(blks[it - 2])
                if 0 <= it - 4 < n:
                    stC(blks[it - 4])
                    bk = blks[it - 4]
                    if bk["first"] and bk["j"] == 0 and 1 <= bk["h"] < H - 1:
                        load_head(bk["h"] + 1)
            k.emit()
            self.stats["s2sb"] = k.n_ops

    def stage2_mla(self):
        cfg, I, Sc = self.cfg, self.I, self.Sc
        TT, NB, S, SO, NBLK = cfg.TT, cfg.NB, cfg.S, cfg.SO, cfg.NBLK
        with ExitStack() as st:
            k, G, P = self.newk(st)
            ones = G["ones"]
            mk = k.sb("mk", [128, 2 * 2 * NB * TT], BF16)
            mkd = self.dr(k, "mki", I["mki"])
            k.dma("sp", lambda e: e.dma_start(out=mk[:], in_=mkd[:]), reads=[mkd], writes=[mk])
            mk4 = mk[:].rearrange("p (s b t) -> p s b t", s=2, b=2 * NB)
            Kn = [k.sb("Kn_%d" % i, [128, S], BF16) for i in range(2)]
            Kp = [k.sb("Kp_%d" % i, [128, S], BF16) for i in range(2)]
            V = [k.sb("V_%d" % i, [128, NBLK, 128], BF16) for i in range(2)]
            Qn = [k.sb("Qn_%d" % i, [128, SO], BF16) for i in range(2)]
            Qp = [k.sb("Qp_%d" % i, [128, SO], BF16) for i in range(2)]
            e_t = [k.sb("e%d" % i, [128, TT], BF16) for i in range(4)]
            rden = k.sb("rden", [128, TT], F32)
            yo = [k.sb("yo%d" % i, [128, TT], BF16) for i in range(2)]
            s_ps = [P[0], P[1], P[2]]
            y_ps = [P[3], P[4]]
            d_ps = [P[5], P[6]]
            knd, kpd, vmd, qnd, qpd, yad = (self.dr(k, n, Sc[n]) for n in ("kn", "kp", "vm", "qn", "qp", "ya"))

            def load_head(h):
                b = h % 2
                k.dma("sp", lambda e: e.dma_start(out=Kn[b][:], in_=knd[h]), reads=[knd], writes=[Kn[b]])
                k.dma("sp", lambda e: e.dma_start(out=Kp[b][:], in_=kpd[h]), reads=[kpd], writes=[Kp[b]])
                k.dma("sp", lambda e: e.dma_start(out=V[b][:], in_=vmd[h]), reads=[vmd], writes=[V[b]])
                k.dma("sp", lambda e: e.dma_start(out=Qn[b][:], in_=qnd[h]), reads=[qnd], writes=[Qn[b]])
                k.dma("sp", lambda e: e.dma_start(out=Qp[b][:], in_=qpd[h]), reads=[qpd], writes=[Qp[b]])

            blks = self.blocks()

            def stA(bk):
                n, h, j, kb = bk["n"], bk["h"], bk["j"], bk["kb"]
                b = h % 2
                sp_, et = s_ps[n % 3], e_t[n % 4]
                k.op("pe", lambda e: e.matmul(out=sp_[:, 0:TT], lhsT=Kn[b][:, kb * 128:(kb + 1) * 128],
                                              rhs=Qn[b][:, j * TT:(j + 1) * TT], start=True, stop=False),
                     reads=[Kn[b], Qn[b]], writes=[sp_])
                k.op("pe", lambda e: e.matmul(out=sp_[:, 0:TT], lhsT=Kp[b][:, kb * 128:(kb + 1) * 128],
                                              rhs=Qp[b][:, j * TT:(j + 1) * TT], start=False, stop=True),
                     reads=[Kp[b], Qp[b]], writes=[sp_])
                k.op("act", lambda e: e.activation(out=et[:], in_=sp_[:, 0:TT], func=AF.Exp, scale=SC_MLA), reads=[sp_], writes=[et])
                if bk["di"] is not None:
                    di, s = bk["di"], j % 2
                    k.op("pool", lambda e: e.tensor_tensor(out=et[:], in0=et[:], in1=mk4[:, s, di, :], op=ALU.mult),
                         reads=[et, mk], writes=[et])

            def stB(bk):
                n, h, j, kb = bk["n"], bk["h"], bk["j"], bk["kb"]
                b = h % 2
                et = e_t[n % 4]
                par = (h * cfg.NO + j) % 2
                yp, dp = y_ps[par], d_ps[par]
                k.op("pe", lambda e: e.matmul(out=yp[:, 0:TT], lhsT=V[b][:, kb, :], rhs=et[:], start=bk["first"], stop=bk["last"]),
                     reads=[V[b], et], writes=[yp])
                k.op("pe", lambda e: e.matmul(out=dp[:, 0:TT], lhsT=ones[:, :], rhs=et[:], start=bk["first"], stop=bk["last"]),
                     reads=[ones, et], writes=[dp])
                if bk["last"]:
                    y = yo[par]
                    k.op("dve", lambda e: e.reciprocal(out=rden[:], in_=dp[:, 0:TT]), reads=[dp], writes=[rden])
                    k.op("dve", lambda e: e.tensor_tensor(out=y[:], in0=yp[:, 0:TT], in1=rden[:], op=ALU.mult),
                         reads=[yp, rden], writes=[y])
                    k.dma("pool", lambda e: e.dma_start(out=yad[h, :, j * TT:(j + 1) * TT], in_=y[:]),
                          reads=[y], writes=[yad], key="sc_y")

            n = len(blks)
            load_head(0)
            load_head(1)
            for it in range(n + 2):
                if it < n:
                    stA(blks[it])
                if 0 <= it - 2 < n:
                    stB(blks[it - 2])
                    bk = blks[it - 2]
                    if bk["first"] and bk["j"] == 0 and 1 <= bk["h"] < H - 1:
                        load_head(bk["h"] + 1)
            k.emit()
            self.stats["s2mla"] = k.n_ops

    def stage3(self):
        cfg, I, Sc = self.cfg, self.I, self.Sc
        TT, NB, SO = cfg.TT, cfg.NB, cfg.SO
        with ExitStack() as st:
            k, G, P = self.newk(st)
            gc, ones = G["gc"], G["ones"]
            need = 2 * D * 4 + NB * D * 2 + D * 2
            big = k.sb("big", [128, max(NFF * TT, need // 2)], BF16)
            bigf = big[:].bitcast(F32)
            B = {}
            o = 0
            xb0 = k.wrap("xblk0", bigf[:, o:o + D], "sb"); o += D
            xb1 = k.wrap("xblk1", bigf[:, o:o + D], "sb"); o += D
            B["xblk"] = [xb0, xb1]
            ob = 2 * o
            xnv = k.wrap("xn", big[:, ob:ob + NB * D].rearrange("p (b f) -> p b f", b=NB), "sb"); ob += NB * D
            junk = k.wrap("junk", big[:, ob:ob + D], "sb"); ob += D
            B["xn"], B["junk"] = xnv, junk
            GT = k.wrap("GT", big[:, 0:NFF * TT].rearrange("p (c t) -> p c t", c=NFF), "sb")
            k.alias(GT, xb0, xb1, xnv, junk)
            for b_ in (xb0, xb1, xnv, junk):
                b_.aliases = [GT]
            GT.aliases = [xb0, xb1, xnv, junk]
            B["ss"] = k.sb("ss", [128, NB], F32)
            B["rs"] = k.sb("rs", [128, NB], F32)
            B["hT"] = k.sb("hT", [128, 16, TT], BF16)
            hT = B["hT"]
            xT = k.sb("xT", [128, 16, TT], F32)
            reg = k.sb("reg", [128, 32 * TT], BF16)
            Ya = k.wrap("Ya", reg[:, 0:8 * TT].rearrange("p (h t) -> p h t", h=8), "sb")
            Yb = k.wrap("Yb", reg[:, 8 * TT:16 * TT].rearrange("p (h t) -> p h t", h=8), "sb")
            mg = k.wrap("mg", reg[:, 16 * TT:32 * TT].rearrange("p (c t) -> p c t", c=16), "sb")
            regf = reg[:].bitcast(F32)
            x2t = k.wrap("x2t", regf[:, 0:4 * TT].rearrange("p (c t) -> p c t", c=4), "sb")
            ost = [k.wrap("ost%d" % i, regf[:, (4 + 4 * i) * TT // 1:(8 + 4 * i) * TT // 1].rearrange("p (b f) -> p b f", b=NB), "sb")
                   for i in range(2)]
            ph1, ph2 = [Ya, Yb, mg], [x2t] + ost
            for b_ in ph1:
                b_.aliases = list(ph2)
            for b_ in ph2:
                b_.aliases = list(ph1)
            sa = [k.sb("sa%d" % i, [128, TT], F32) for i in range(2)]
            sb_ = [k.sb("sb%d" % i, [128, TT], F32) for i in range(2)]
            m1 = [k.sb("m1%d" % i, [128, TT], F32) for i in range(2)]
            m2 = [k.sb("m2%d" % i, [128, TT], F32) for i in range(2)]
            sq2 = [k.sb("sq2%d" % i, [128, TT], BF16) for i in range(2)]
            R2 = k.sb("R2", [128, TT], F32)
            tmp = [k.sb("tmp%d" % i, [128, TT], F32) for i in range(2)]
            sg = [k.sb("sg%d" % i, [128, TT], F32) for i in range(2)]
            NW = 4
            wst = [k.sb("wst%d" % i, [128, 2048], F32) for i in range(NW)]
            wbf = [k.sb("wbf%d" % i, [128, 2048], BF16) for i in range(NW)]
            wc = [0]
            xo = self.dr(k, "xo", I["xo"])
            outd = self.dr(k, "out", self.out)
            yad, ybd = self.dr(k, "ya", Sc["ya"]), self.dr(k, "yb", Sc["yb"])
            W = {n: self.dr(k, n, I[n]) for n in ("w_ga", "w_gb", "w_pa", "w_pb", "w_o", "w_f1", "w_f2")}
            tpb = [P[0], P[1]]
            tpf = [P[2], P[3]]
            accs = [P[2], P[3], P[4], P[5], P[6], P[7], P[0]]
            ai = [0]

            def nacc():
                ai[0] += 1
                return accs[ai[0] % len(accs)]

            woff = {"w_ga": 0, "w_gb": 16, "w_pa": 32, "w_pb": 48, "w_o": 64, "w_f1": 80, "w_f2": 168}
            wsc = self.dr(k, "s_wbf", self.nc.dram_tensor("s_wbf", [232, 128, 2048], BF16, kind="Internal").ap())
            cur_tile = [0]

            def wunit(name, u, ncols):
                i = wc[0] % NW
                wc[0] += 1
                s, b = wst[i], wbf[i]
                wd = W[name]
                gu = woff[name] + u
                if cur_tile[0] > 0:
                    k.dma("sp", lambda e: e.dma_start(out=b[:, 0:ncols], in_=wsc[gu, :, 0:ncols]), reads=[wsc], writes=[b])
                    return b
                k.dma("sp", lambda e: e.dma_start(out=s[:, 0:ncols], in_=wd[u]), reads=[wd], writes=[s])
                eng = ("dve", "act", "dve", "pool")[wc[0] % 4]
                if eng == "act":
                    k.op("act", lambda e: e.activation(out=b[:, 0:ncols], in_=s[:, 0:ncols], func=AF.Copy), reads=[s], writes=[b])
                else:
                    k.op(eng, lambda e: e.tensor_copy(out=b[:, 0:ncols], in_=s[:, 0:ncols]), reads=[s], writes=[b])
                k.dma("pool", lambda e: e.dma_start(out=wsc[gu, :, 0:ncols], in_=b[:, 0:ncols]), reads=[b], writes=[wsc], key="wsc")
                return b

            def group(ps, wb, nch, rhs_fn, rbufs, start=True, stop=True):
                for c in range(nch):
                    k.op("pe", lambda e, c=c: e.matmul(out=ps[:, 0:TT], lhsT=wb[:, c * 128:(c + 1) * 128], rhs=rhs_fn(c),
                                                       start=(start and c == 0), stop=(stop and c == nch - 1)),
                         reads=[wb] + rbufs, writes=[ps])

            for j in range(cfg.NO):
                t0 = j * TT
                cur_tile[0] = j
                self.prologue(k, G, xo, t0, B, tpb, f32T=xT, tpf=tpf)
                k.dma("sp", lambda e, t0=t0: e.dma_start(out=Ya[:], in_=yad[:, :, t0:t0 + TT].rearrange("h p t -> p h t")),
                      reads=[yad], writes=[Ya])
                k.dma("sp", lambda e, t0=t0: e.dma_start(out=Yb[:], in_=ybd[:, :, t0:t0 + TT].rearrange("h p t -> p h t")),
                      reads=[ybd], writes=[Yb])
                for nb in range(16):
                    i2 = nb % 2
                    wga = wunit("w_ga", nb, 2048)
                    pga = nacc()
                    group(pga, wga, 16, lambda c: hT[:, c, :], [hT])
                    k.op("act", lambda e, pga=pga, i2=i2: e.activation(out=sa[i2][:], in_=pga[:, 0:TT], func=AF.Sigmoid),
                         reads=[pga], writes=[sa[i2]])
                    wgb = wunit("w_gb", nb, 2048)
                    pgb = nacc()
                    group(pgb, wgb, 16, lambda c: hT[:, c, :], [hT])
                    k.op("act", lambda e, pgb=pgb, i2=i2: e.activation(out=sb_[i2][:], in_=pgb[:, 0:TT], func=AF.Sigmoid),
                         reads=[pgb], writes=[sb_[i2]])
                    wpa = wunit("w_pa", nb, 1024)
                    ppa = nacc()
                    group(ppa, wpa, 8, lambda c: Ya[:, c, :], [Ya])
                    k.op("dve", lambda e, ppa=ppa, i2=i2: e.tensor_tensor(out=m1[i2][:], in0=ppa[:, 0:TT], in1=sa[i2][:], op=ALU.mult),
                         reads=[ppa, sa[i2]], writes=[m1[i2]])
                    wpb = wunit("w_pb", nb, 1024)
                    ppb = nacc()
                    group(ppb, wpb, 8, lambda c: Yb[:, c, :], [Yb])
                    k.op("dve", lambda e, ppb=ppb, i2=i2: e.tensor_tensor(out=m2[i2][:], in0=ppb[:, 0:TT], in1=sb_[i2][:], op=ALU.mult),
                         reads=[ppb, sb_[i2]], writes=[m2[i2]])
                    k.op("pool", lambda e, nb=nb, i2=i2: e.tensor_tensor(out=mg[:, nb, :], in0=m1[i2][:], in1=m2[i2][:], op=ALU.add),
                         reads=[m1[i2], m2[i2]], writes=[mg])
                if cfg.debug:
                    dd = self.dr(k, "s_dmg" + str(j), Sc["dmg"])
                    k.dma("pool", lambda e, dd=dd, j=j: e.dma_start(out=dd[j], in_=mg[:]), reads=[mg], writes=[dd], key="dbg")
                pss = P[1]
                for mb in range(16):
                    wo = wunit("w_o", mb, 2048)
                    po_ = nacc()
                    group(po_, wo, 16, lambda c: mg[:, c, :], [mg])
                    k.op("dve", lambda e, po_=po_, mb=mb: e.scalar_tensor_tensor(
                        out=xT[:, mb, :], in0=po_[:, 0:TT], scalar=gc[:, 32 + mb:33 + mb], in1=xT[:, mb, :],
                        op0=ALU.mult, op1=ALU.add), reads=[po_, gc, xT], writes=[xT])
                    s2 = sq2[mb % 2]
                    k.op("act", lambda e, s2=s2, mb=mb: e.activation(out=s2[:], in_=xT[:, mb, :], func=AF.Square),
                         reads=[xT], writes=[s2])
                    k.op("pe", lambda e, s2=s2, mb=mb: e.matmul(out=pss[:, 0:TT], lhsT=ones[:, :], rhs=s2[:],
                                                                start=(mb == 0), stop=(mb == 15)),
                         reads=[ones, s2], writes=[pss])
                self.rstd_from_ps(k, pss, R2, D, TT)
                for mb in range(16):
                    t_ = tmp[mb % 2]
                    k.op("dve", lambda e, t_=t_, mb=mb: e.scalar_tensor_tensor(
                        out=t_[:], in0=xT[:, mb, :], scalar=gc[:, 48 + mb:49 + mb], in1=R2[:], op0=ALU.mult, op1=ALU.mult),
                        reads=[xT, gc, R2], writes=[t_])
                    k.op("act", lambda e, t_=t_, mb=mb: e.activation(out=hT[:, mb, :], in_=t_[:], func=AF.Identity,
                                                                     bias=gc[:, 64 + mb:65 + mb]),
                         reads=[t_, gc], writes=[hT])
                for fb in range(NFF):
                    wg = wunit("w_f1", fb, 2048)
                    pg = nacc()
                    group(pg, wg, 16, lambda c: hT[:, c, :], [hT])
                    s_ = sg[fb % 2]
                    k.op("act", lambda e, pg=pg, s_=s_: e.activation(out=s_[:], in_=pg[:, 0:TT], func=AF.Silu), reads=[pg], writes=[s_])
                    wu = wunit("w_f1", NFF + fb, 2048)
                    pu = nacc()
                    group(pu, wu, 16, lambda c: hT[:, c, :], [hT])
                    k.op("dve", lambda e, pu=pu, s_=s_, fb=fb: e.tensor_tensor(out=GT[:, fb, :], in0=pu[:, 0:TT], in1=s_[:], op=ALU.mult),
                         reads=[pu, s_], writes=[GT])
                if cfg.debug:
                    for nm, src in (("dx1", xT), ("dG", GT), ("dh2", hT)):
                        dd = self.dr(k, "s_" + nm + str(j), Sc[nm])
                        k.dma("pool", lambda e, dd=dd, src=src, j=j: e.dma_start(out=dd[j], in_=src[:]),
                              reads=[src], writes=[dd], key="dbg")
                for mg4 in range(4):
                    for q in range(4):
                        mb = mg4 * 4 + q
                        pf = nacc()
                        for part in range(4):
                            w2 = wunit("w_f2", mb * 4 + part, 11 * 128)
                            group(pf, w2, 11, lambda c, part=part: GT[:, part * 11 + c, :], [GT],
                                  start=(part == 0), stop=(part == 3))
                        k.op("dve", lambda e, pf=pf, mb=mb, q=q: e.scalar_tensor_tensor(
                            out=x2t[:, q, :], in0=pf[:, 0:TT], scalar=gc[:, 80 + mb:81 + mb], in1=xT[:, mb, :],
                            op0=ALU.mult, op1=ALU.add), reads=[pf, gc, xT], writes=[x2t])
                    os_ = ost[mg4 % 2]
                    for blk in range(NB):
                        pt = tpf[blk % 2]
                        for q in range(4):
                            k.op("pe", lambda e, pt=pt, q=q, blk=blk: e.transpose(
                                out=pt[:, q * 128:(q + 1) * 128], in_=x2t[:, q, blk * 128:(blk + 1) * 128], identity=G["idf"][:]),
                                reads=[x2t, G["idf"]], writes=[pt])
                        k.op("act", lambda e, pt=pt, os_=os_, blk=blk: e.activation(out=os_[:, blk, :], in_=pt[:, 0:512], func=AF.Copy),
                             reads=[pt], writes=[os_])
                    k.dma("pool", lambda e, os_=os_, mg4=mg4, t0=t0: e.dma_start(
                        out=outd[t0:t0 + TT, mg4 * 512:(mg4 + 1) * 512].rearrange("(b p) f -> p b f", p=128), in_=os_[:]),
                        reads=[os_], writes=[outd], key="out")
            k.emit()
            self.stats["s3"] = k.n_ops


def lay(w):
    Kd, N = w.shape
    C = Kd // 128
    return np.ascontiguousarray(w.reshape(C, 128, N).transpose(1, 0, 2).reshape(128, C * N))


def lay_units(w, ncol=128):
    Kd, N = w.shape
    C = Kd // 128
    U = N // ncol
    return np.ascontiguousarray(w.reshape(C, 128, U, ncol).transpose(2, 1, 0, 3).reshape(U, 128, C * ncol))


def fm(v):
    return np.ascontiguousarray(v.reshape(-1, 128).T)


def make_masks(cfg, half, strict):
    TT, NB = cfg.TT, cfg.NB
    out = np.zeros((128, 2, 2 * NB, TT), np.float32)
    p = np.arange(128)[:, None, None]
    b = np.arange(2 * NB)[None, :, None]
    t = np.arange(TT)[None, None, :]
    key = b * 128 + p
    for s in range(2):
        qoff = TT * ((s == 1) if half == 0 else (s == 0))
        qry = qoff + t
        out[:, s] = (key < qry) if strict else (key <= qry)
    return out.reshape(128, -1).astype(ml_dtypes.bfloat16)


def prep_shared(cfg, inp):
    f32 = np.float32
    w_in = np.asarray(inp["w_in"][0], f32)
    offs = np.cumsum([0, 512, 512, 64, 1024, 1024, 1024, 2048, 2048])
    c_q, c_kv, k_pe, q_sb, k_sb, v_sb, gl_a, gl_b = [w_in[:, offs[i]:offs[i + 1]] for i in range(8)]
    Wd = {}
    Wd["w_ada"] = lay_units(np.asarray(inp["w_ada"][0], f32))
    Wd["w_ksb"] = lay(k_sb)
    Wd["w_vsb"] = lay(v_sb)
    Wd["w_qsb"] = lay(q_sb)
    Wd["w_ckv"] = lay(c_kv)
    kpe_sw = np.concatenate([k_pe[:, 32:64], k_pe[:, 0:32]], axis=1)
    Wd["w_kpe"] = lay(np.concatenate([k_pe, kpe_sw, kpe_sw, k_pe], axis=1))
    Wd["w_cq"] = lay(c_q)
    w_ukv = np.asarray(inp["w_ukv"][0], f32).reshape(512, H, 256)
    Wd["w_ukn"] = lay(np.ascontiguousarray(w_ukv[:, :, 0:128]).reshape(512, 1024))
    Wd["w_ukv"] = lay(np.ascontiguousarray(w_ukv[:, :, 128:256]).reshape(512, 1024))
    w_uq = np.asarray(inp["w_uq"][0], f32).reshape(512, H, 192)
    pe_, pesw_ = w_uq[:, :, 128:192], np.concatenate([w_uq[:, :, 160:192], w_uq[:, :, 128:160]], axis=2)
    uq = np.concatenate([w_uq[:, :, 0:128], pe_, pesw_, pesw_, pe_], axis=2)
    Wd["w_uq"] = lay(np.ascontiguousarray(uq).reshape(512, 3072))
    Wd["w_ga"] = lay_units(gl_a)
    Wd["w_gb"] = lay_units(gl_b)
    Wd["w_pa"] = lay_units(np.asarray(inp["w_proj_mla"][0], f32))
    Wd["w_pb"] = lay_units(np.asarray(inp["w_proj_sb"][0], f32))
    Wd["w_o"] = lay_units(np.asarray(inp["w_out"][0], f32))
    Wd["w_f1"] = lay_units(np.asarray(inp["w_ffn_in"][0], f32))
    w2 = np.asarray(inp["w_ffn_out"][0], f32)
    w2r = w2.reshape(4, 11, 128, 16, 128).transpose(3, 0, 2, 1, 4).reshape(64, 128, 11 * 128)
    Wd["w_f2"] = np.ascontiguousarray(w2r)
    Wd["idf"] = np.eye(128, dtype=f32)
    jj = np.arange(128)[:, None]
    ss = np.arange(128)[None, :]
    Wd["tri"] = (jj >= ss).astype(ml_dtypes.bfloat16)
    cvb = np.zeros((128, NCV), f32)

    def put(name, arr):
        a, b = CV[name]
        cvb[:arr.shape[0], a:b] = arr
    put("bada", fm(np.asarray(inp["b_ada"][0], f32)))
    put("g1", fm(np.asarray(inp["g_norm1"][0], f32)))
    put("g2", fm(np.asarray(inp["g_norm2"][0], f32)))
    put("gq", fm(np.asarray(inp["g_q_latent"][0], f32)))
    put("gkv", fm(np.asarray(inp["g_kv_latent"][0], f32)))
    gqh = np.asarray(inp["g_q_head"][0], f32)
    gkh = np.asarray(inp["g_k_head"][0], f32)
    put("gqh", gqh[0:128, None])
    put("gkh", gkh[0:128, None])
    put("gqpe", gqh[128:192, None])
    put("gqpes", np.concatenate([gqh[160:192], gqh[128:160]])[:, None])
    put("gkpe", gkh[128:192, None])
    put("gkpes", np.concatenate([gkh[160:192], gkh[128:160]])[:, None])
    fr = (np.float32(10000.0) ** (-np.arange(32, dtype=f32) / np.float32(32))).astype(f32)
    put("freq", np.concatenate([fr, fr])[:, None])
    Wd["_cv"] = cvb
    return Wd


def prep_core(cfg, inp, Wd, core):
    f32 = np.float32
    b, half = core // 2, core % 2
    TT = cfg.TT
    x = np.asarray(inp["x"][b], f32)
    pos = np.asarray(inp["positions"][b], np.int32)
    own = cfg.own[half]
    rows = np.concatenate([np.arange(t * TT, (t + 1) * TT) for t in own])
    cvb = Wd["_cv"].copy()
    a, e = CV["c"]
    cvb[:, a:e] = fm(np.asarray(inp["c"][b], f32))
    m = {k_: v for k_, v in Wd.items() if not k_.startswith("_")}
    m["xa"] = x
    m["xo"] = np.ascontiguousarray(x[rows])
    m["pa"] = np.ascontiguousarray(np.broadcast_to(pos[None, :], (64, pos.shape[0])))
    m["po"] = np.ascontiguousarray(np.broadcast_to(pos[rows][None, :], (64, rows.shape[0])))
    m["cv"] = cvb
    m["mks"] = make_masks(cfg, half, True)
    m["mki"] = make_masks(cfg, half, False)
    return m, rows


_CACHE = {}


def run(cfg, inputs):
    key = (cfg.S, cfg.TT, cfg.debug, getattr(cfg, "stages", "012345"))
    if key not in _CACHE:
        p = Prog(cfg)
        p.build()
        _CACHE[key] = p
    p = _CACHE[key]
    Wd = prep_shared(cfg, inputs)
    in_maps, rows_all = [], []
    for core in range(8):
        m, rows = prep_core(cfg, inputs, Wd, core)
        in_maps.append(m)
        rows_all.append(rows)
    used = set(p.I.keys())
    in_maps = [{k_: v for k_, v in m.items() if k_ in used} for m in in_maps]
    res = run_bass_kernel_spmd(p.nc, in_maps, core_ids=list(range(8)))
    return res, rows_all


def kernel(**inputs):
    cfg = Cfg()
    res, rows_all = run(cfg, inputs)
    B_ = inputs["x"].shape[0]
    out = np.zeros((B_, cfg.S, D), np.float32)
    for core in range(8):
        out[core // 2, rows_all[core]] = res.results[core]["out"]
    return out
```
